# Optimizing a Trainium2 kernel written in Bass

```python
import jax, jax.numpy as jnp
from jax import lax
import numpy as np

D_MODEL = 1024
BATCH = 8
SEQ = 8192
DEPTH = 2
DEC_BATCH = 8
DEC_SEQ = 4096
PAST_LEN = 128

MLA_HEADS = 16
Q_LORA = 256
KV_LORA = 128
QK_NOPE = 64
QK_ROPE = 32
V_DIM = 64
ROPE_THETA = 10000.0
Q_BLOCK = 128
NA_HEADS = 16
NA_HEAD_DIM = D_MODEL // NA_HEADS
GRID_W = 64
WIN_R = 8
WIN_C = 16
FFN_HIDDEN = -(-8 * D_MODEL // (3 * 256)) * 256
N_MLA = (DEPTH + 1) // 2
N_NA = DEPTH // 2
N_MOD = 6
RMS_EPS = 1e-6
NEG_INF = -1e30

kernel_name = 'hybrid_mla_natten_encoder'


def _rmsnorm(x, g):
    xf = x.astype(jnp.float32)
    y = xf * lax.rsqrt(jnp.mean(xf * xf, axis=-1, keepdims=True) + RMS_EPS)
    return (y * g.astype(jnp.float32)).astype(x.dtype)


def _rope_tables(seq_len):
    inv_freq = 1.0 / (ROPE_THETA ** (jnp.arange(0, QK_ROPE, 2, dtype=jnp.float32) / QK_ROPE))
    ang = jnp.arange(seq_len, dtype=jnp.float32)[:, None] * inv_freq[None, :]
    return jnp.cos(ang), jnp.sin(ang)


def _apply_rope(x, cos, sin):
    cos = cos.astype(x.dtype)
    sin = sin.astype(x.dtype)
    x1, x2 = jnp.split(x, 2, axis=-1)
    return jnp.concatenate([x1 * cos - x2 * sin, x2 * cos + x1 * sin], axis=-1)


def _mla(h, w_dkv, q_norm, kv_norm, w_uq, w_ukv, w_o):
    B, S, _ = h.shape
    lat = h @ w_dkv
    cq = _rmsnorm(lat[..., :Q_LORA], q_norm)
    ckv = _rmsnorm(lat[..., Q_LORA:Q_LORA + KV_LORA], kv_norm)
    k_rope = lat[..., Q_LORA + KV_LORA:]
    q = (cq @ w_uq).reshape(B, S, MLA_HEADS, QK_NOPE + QK_ROPE)
    kv = (ckv @ w_ukv).reshape(B, S, MLA_HEADS, QK_NOPE + V_DIM)
    k_nope, v = kv[..., :QK_NOPE], kv[..., QK_NOPE:]
    cos, sin = _rope_tables(S)
    scale = (QK_NOPE + QK_ROPE) ** -0.5
    q_nope = q[..., :QK_NOPE] * scale
    q_rope = _apply_rope(q[..., QK_NOPE:], cos[:, None, :], sin[:, None, :]) * scale
    k_rope = _apply_rope(k_rope, cos, sin)
    nb = S // Q_BLOCK
    qn = q_nope.reshape(B, nb, Q_BLOCK, MLA_HEADS, QK_NOPE).transpose(1, 0, 2, 3, 4)
    qr = q_rope.reshape(B, nb, Q_BLOCK, MLA_HEADS, QK_ROPE).transpose(1, 0, 2, 3, 4)

    def attend(blk):
        qn_b, qr_b = blk
        s = (jnp.einsum('bqhd,bkhd->bhqk', qn_b, k_nope, preferred_element_type=jnp.float32)
             + jnp.einsum('bqhr,bkr->bhqk', qr_b, k_rope, preferred_element_type=jnp.float32))
        p = jax.nn.softmax(s, axis=-1).astype(v.dtype)
        return jnp.einsum('bhqk,bkhd->bqhd', p, v)

    o = lax.map(attend, (qn, qr))
    o = o.transpose(1, 0, 2, 3, 4).reshape(B, S, MLA_HEADS * V_DIM)
    return o @ w_o


def _neighbourhood_attention(h, w_qkv, rpb, w_o):
    B, S, _ = h.shape
    rows = S // GRID_W
    win_r = min(WIN_R, rows)
    qkv = (h @ w_qkv).reshape(B, rows, GRID_W, 3, NA_HEADS, NA_HEAD_DIM)
    q = qkv[:, :, :, 0] * (NA_HEAD_DIM ** -0.5)
    k = qkv[:, :, :, 1]
    v = qkv[:, :, :, 2]
    c_idx = jnp.arange(GRID_W)
    c_start = jnp.clip(c_idx - WIN_C // 2, 0, GRID_W - WIN_C)
    col_valid = (c_idx[None, :] >= c_start[:, None]) & (c_idx[None, :] < c_start[:, None] + WIN_C)
    dc_idx = jnp.clip(c_idx[None, :] - c_idx[:, None] + WIN_C - 1, 0, 2 * WIN_C - 2)
    col_bias = rpb.astype(jnp.float32)[:, :, dc_idx]
    col_bias = jnp.where(col_valid[None, None], col_bias, NEG_INF)

    def row_block(r):
        r_start = jnp.clip(r - win_r // 2, 0, rows - win_r)
        q_r = lax.dynamic_index_in_dim(q, r, axis=1, keepdims=False)
        k_band = lax.dynamic_slice_in_dim(k, r_start, win_r, axis=1)
        v_band = lax.dynamic_slice_in_dim(v, r_start, win_r, axis=1)
        dr_idx = r_start + jnp.arange(win_r) - r + WIN_R - 1
        bias = col_bias[:, dr_idx].transpose(0, 2, 1, 3)
        s = jnp.einsum('bqhd,bjkhd->bhqjk', q_r, k_band, preferred_element_type=jnp.float32) + bias[None]
        p = jax.nn.softmax(s.reshape(B, NA_HEADS, GRID_W, win_r * GRID_W), axis=-1)
        p = p.reshape(B, NA_HEADS, GRID_W, win_r, GRID_W).astype(v.dtype)
        return jnp.einsum('bhqjk,bjkhd->bqhd', p, v_band)

    o = lax.map(row_block, jnp.arange(rows, dtype=jnp.int32))
    o = o.transpose(1, 0, 2, 3, 4).reshape(B, S, NA_HEADS * NA_HEAD_DIM)
    return o @ w_o


def _swiglu(h, w_gu, w_down):
    gu = h @ w_gu
    g, u = gu[..., :FFN_HIDDEN], gu[..., FFN_HIDDEN:]
    return (jax.nn.silu(g) * u) @ w_down


def _trunk(x, c, ada_w, ada_b, norm_pre_mix, norm_post_mix, norm_pre_ffn, norm_post_ffn,
           mla_w_dkv, mla_q_norm, mla_kv_norm, mla_w_uq, mla_w_ukv, mla_w_o,
           na_w_qkv, na_rpb, na_w_o, ffn_w_gu, ffn_w_down):
    B = x.shape[0]
    c_act = jax.nn.silu(c)
    for i in range(DEPTH):
        mod = (c_act @ ada_w[i] + ada_b[i]).astype(x.dtype).reshape(B, N_MOD, 1, D_MODEL)
        shift_m, scale_m, gate_m = mod[:, 0], mod[:, 1], mod[:, 2]
        shift_f, scale_f, gate_f = mod[:, 3], mod[:, 4], mod[:, 5]
        h = _rmsnorm(x, norm_pre_mix[i]) * (1 + scale_m) + shift_m
        j = i // 2
        if i % 2 == 0:
            h = _mla(h, mla_w_dkv[j], mla_q_norm[j], mla_kv_norm[j], mla_w_uq[j], mla_w_ukv[j], mla_w_o[j])
        else:
            h = _neighbourhood_attention(h, na_w_qkv[j], na_rpb[j], na_w_o[j])
        x = x + gate_m * _rmsnorm(h, norm_post_mix[i])
        h = _rmsnorm(x, norm_pre_ffn[i]) * (1 + scale_f) + shift_f
        h = _swiglu(h, ffn_w_gu[i], ffn_w_down[i])
        x = x + gate_f * _rmsnorm(h, norm_post_ffn[i])
    return x


def _w(k, shape, fan_in, gain=1.0):
    return (gain * fan_in ** -0.5) * jax.random.normal(k, shape, dtype=jnp.float32)


def _gain(k, shape):
    return 1.0 + 0.05 * jax.random.normal(k, shape, dtype=jnp.float32)


def setup_inputs(seed: int = 0) -> dict:
    key = jax.random.key(seed)
    ks = jax.random.split(key, 24)
    D = D_MODEL
    return {
        'x_prompt': jax.random.normal(ks[0], (BATCH, SEQ, D), dtype=jnp.float32),
        'x_sample': jax.random.normal(ks[1], (DEC_BATCH, DEC_SEQ, D), dtype=jnp.float32),
        'c_prompt': jax.random.normal(ks[2], (BATCH, D), dtype=jnp.float32),
        'c_sample': jax.random.normal(ks[3], (DEC_BATCH, D), dtype=jnp.float32),
        'ada_w': _w(ks[4], (DEPTH, D, N_MOD * D), D, 0.5),
        'ada_b': 0.02 * jax.random.normal(ks[5], (DEPTH, N_MOD * D), dtype=jnp.float32),
        'norm_pre_mix': _gain(ks[6], (DEPTH, D)),
        'norm_post_mix': _gain(ks[7], (DEPTH, D)),
        'norm_pre_ffn': _gain(ks[8], (DEPTH, D)),
        'norm_post_ffn': _gain(ks[9], (DEPTH, D)),
        'mla_w_dkv': _w(ks[10], (N_MLA, D, Q_LORA + KV_LORA + QK_ROPE), D),
        'mla_q_norm': _gain(ks[11], (N_MLA, Q_LORA)),
        'mla_kv_norm': _gain(ks[12], (N_MLA, KV_LORA)),
        'mla_w_uq': _w(ks[13], (N_MLA, Q_LORA, MLA_HEADS * (QK_NOPE + QK_ROPE)), Q_LORA),
        'mla_w_ukv': _w(ks[14], (N_MLA, KV_LORA, MLA_HEADS * (QK_NOPE + V_DIM)), KV_LORA),
        'mla_w_o': _w(ks[15], (N_MLA, MLA_HEADS * V_DIM, D), MLA_HEADS * V_DIM),
        'na_w_qkv': _w(ks[16], (N_NA, D, 3 * NA_HEADS * NA_HEAD_DIM), D),
        'na_rpb': 0.1 * jax.random.normal(ks[17], (N_NA, NA_HEADS, 2 * WIN_R - 1, 2 * WIN_C - 1), dtype=jnp.float32),
        'na_w_o': _w(ks[18], (N_NA, NA_HEADS * NA_HEAD_DIM, D), NA_HEADS * NA_HEAD_DIM),
        'ffn_w_gu': _w(ks[19], (DEPTH, D, 2 * FFN_HIDDEN), D),
        'ffn_w_down': _w(ks[20], (DEPTH, FFN_HIDDEN, D), FFN_HIDDEN),
    }


def reference(x_prompt, x_sample, c_prompt, c_sample, ada_w, ada_b, norm_pre_mix, norm_post_mix,
              norm_pre_ffn, norm_post_ffn, mla_w_dkv, mla_q_norm, mla_kv_norm, mla_w_uq, mla_w_ukv,
              mla_w_o, na_w_qkv, na_rpb, na_w_o, ffn_w_gu, ffn_w_down):
    y_prompt = _trunk(x_prompt, c_prompt, ada_w, ada_b, norm_pre_mix, norm_post_mix, norm_pre_ffn,
                      norm_post_ffn, mla_w_dkv, mla_q_norm, mla_kv_norm, mla_w_uq, mla_w_ukv, mla_w_o,
                      na_w_qkv, na_rpb, na_w_o, ffn_w_gu, ffn_w_down)
    y_sample = _trunk(x_sample, c_sample, ada_w, ada_b, norm_pre_mix, norm_post_mix, norm_pre_ffn,
                      norm_post_ffn, mla_w_dkv, mla_q_norm, mla_kv_norm, mla_w_uq, mla_w_ukv, mla_w_o,
                      na_w_qkv, na_rpb, na_w_o, ffn_w_gu, ffn_w_down)
    return (y_prompt, y_sample)
```

```python
import contextlib
import numpy as np
import ml_dtypes
import concourse.bass as bass
import concourse.mybir as mybir
from concourse.bass_utils import run_bass_kernel_spmd

F32 = mybir.dt.float32
BF16 = mybir.dt.bfloat16
AF = mybir.ActivationFunctionType
ALU = mybir.AluOpType

D = 1024
KD = 8
H = 16
QL = 256
KVL = 128
FH = 2816
NJ = 22
EPS = 1e-6
MLA_SCALE = 96 ** -0.5
NA_SCALE = 64 ** -0.5
GRID_W = 64
WIN_R = 8
WIN_C = 16
MASKV = -1000.0
NCORES = 8


_UID = [0]


def _uid():
    _UID[0] += 1
    return _UID[0]


class Res:
    __slots__ = ("last_w", "rd_eng", "rd_dma")

    def __init__(self):
        self.last_w = None
        self.rd_eng = {}
        self.rd_dma = []


class _Op:
    __slots__ = ("eng", "fn", "deps", "is_dma", "sem", "cnt")


ENGS = ["pe", "act", "dve", "pool", "sp"]
BLOCKNAME = {"pe": "tensor", "act": "scalar", "dve": "vector", "pool": "gpsimd", "sp": "sync"}
NDMASEM = {"sp": 16, "pool": 2, "act": 4}


class Prog:
    def __init__(self, nc):
        self.nc = nc
        self.ops = []
        self.per_eng = {e: [] for e in ENGS}
        self.dma_ids = {q: [] for q in NDMASEM}
        self.all_dma = []

    def op(self, eng, fn, r=(), w=(), dma=False, extra=()):
        o = _Op()
        o.eng, o.fn, o.is_dma, o.sem, o.cnt = eng, fn, dma, None, 0
        idx = len(self.ops)
        deps = set(extra)
        for x in r:
            if x.last_w is not None:
                deps.add(x.last_w)
        for x in w:
            if x.last_w is not None:
                deps.add(x.last_w)
            deps.update(x.rd_eng.values())
            deps.update(x.rd_dma)
        for x in r:
            if dma:
                x.rd_dma.append(idx)
            else:
                x.rd_eng[eng] = idx
        for x in w:
            x.last_w = idx
            x.rd_eng = {}
            x.rd_dma = []
        if dma:
            hist = self.dma_ids[eng]
            k = NDMASEM[eng]
            o.sem = (eng, len(hist) % k)
            o.cnt = 16 * (len(hist) // k + 1)
            if len(hist) >= k:
                deps.add(hist[len(hist) - k])
            hist.append(idx)
            self.all_dma.append(idx)
        deps.discard(idx)
        o.deps = deps
        self.ops.append(o)
        self.per_eng[eng].append(idx)
        return idx

    def dma(self, eng, out, in_, r=(), w=()):
        return self.op(eng, lambda e: e.dma_start(out=out, in_=in_), r, w, dma=True)

    def mm(self, out, lhsT, rhs, start=True, stop=True, r=(), w=(), skip=False):
        return self.op("pe", lambda e: e.matmul(out, lhsT, rhs, start=start, stop=stop, skip_group_check=skip), r, w)

    def tr(self, out, in_, ident, r=(), w=()):
        return self.op("pe", lambda e: e.transpose(out, in_, ident), r, w)

    def act(self, out, in_, func, r=(), w=(), **kw):
        return self.op("act", lambda e: e.activation(out, in_, func, **kw), r, w)

    def tt(self, eng, out, in0, in1, op, r=(), w=()):
        return self.op(eng, lambda e: e.tensor_tensor(out, in0, in1, op), r, w)

    def stt(self, out, in0, scalar, in1, op0, op1, r=(), w=()):
        return self.op("dve", lambda e: e.scalar_tensor_tensor(out, in0, scalar, in1, op0, op1), r, w)

    def ts(self, eng, out, in0, s1, s2, op0, op1=None, r=(), w=()):
        if op1 is None:
            return self.op(eng, lambda e: e.tensor_scalar(out, in0, s1, None, op0), r, w)
        return self.op(eng, lambda e: e.tensor_scalar(out, in0, s1, s2, op0, op1), r, w)

    def cp(self, eng, out, in_, r=(), w=()):
        if eng == "act":
            return self.op(eng, lambda e: e.copy(out, in_), r, w)
        return self.op(eng, lambda e: e.tensor_copy(out, in_), r, w)

    def recip(self, out, in_, r=(), w=()):
        return self.op("dve", lambda e: e.reciprocal(out, in_), r, w)

    def memset(self, eng, ap, val, w=()):
        return self.op(eng, lambda e: e.memset(ap, val), (), w)

    def emit(self):
        nc = self.nc
        ops = self.ops
        self.op("sp", None, extra=list(self.all_dma))
        flagged = set()
        for o in ops:
            flagged.update(o.deps)
        cnt = {e: 0 for e in ENGS}
        for i, o in enumerate(ops):
            if not o.is_dma and i in flagged:
                cnt[o.eng] += 1
                o.cnt = cnt[o.eng]
                o.sem = (o.eng, -1)
        sems = {}
        for e in ENGS:
            sems[(e, -1)] = nc.alloc_semaphore(name="se%d" % _uid())
        for q, k in NDMASEM.items():
            for s in range(min(k, len(self.dma_ids[q]))):
                sems[(q, s)] = nc.alloc_semaphore(name="sd%d" % _uid())
        with nc.Block() as block:
            for e in ENGS:
                ids = self.per_eng[e]
                if not ids:
                    continue

                def body(engine, e=e, ids=ids):
                    wm = {}
                    for i in ids:
                        o = ops[i]
                        need = {}
                        for d in o.deps:
                            od = ops[d]
                            if e == "pe" and od.eng == "pe" and not od.is_dma and not o.is_dma:
                                continue
                            if need.get(od.sem, 0) < od.cnt:
                                need[od.sem] = od.cnt
                        for s, c in need.items():
                            if wm.get(s, 0) < c:
                                engine.wait_ge(sems[s], c)
                                wm[s] = c
                        if o.fn is None:
                            continue
                        inst = o.fn(engine)
                        if o.is_dma:
                            inst.then_inc(sems[o.sem], 16)
                        elif i in flagged:
                            inst.then_inc(sems[o.sem], 1)

                getattr(block, BLOCKNAME[e])(body)
        nc.clear_and_free_semaphores(list(sems.values()))
        nc.all_engine_barrier()


class Ctx:
    def __init__(self, nc, es):
        self.nc, self.es, self.n = nc, es, 0

    def sb(self, shape, dt):
        self.n += 1
        return self.es.enter_context(self.nc.sbuf_tensor("sb%d" % _uid(), list(shape), dt))

    def ps(self, bf=False):
        self.n += 1
        if bf:
            return self.es.enter_context(self.nc.psum_tensor("ps%d" % _uid(), [128, 1024], BF16))
        return self.es.enter_context(self.nc.psum_tensor("ps%d" % _uid(), [128, 512], F32))

    def ps2(self):
        self.n += 1
        return self.es.enter_context(self.nc.psum_tensor("ps%d" % _uid(), [128, 1024], F32))


def rstd_ops(P, ss, v, rstd, nh, inv_n, Rss, Rv, Rrs, Rnh):
    P.ts("pool", v, ss, inv_n, EPS, ALU.mult, ALU.add, r=[Rss], w=[Rv])
    P.tt("pool", rstd, v, nh, ALU.pow, r=[Rv, Rnh], w=[Rrs])


def load_gain(P, C, row_ap, n=D):
    t = C.sb([128, n], F32)
    R = Res()
    P.dma("sp", t[:], row_ap.broadcast_to([128, n]), w=[R])
    return t, R


def phase0(nc, Dm, casts):
    with contextlib.ExitStack() as es:
        C = Ctx(nc, es)
        P = Prog(nc)
        for dst, src in casts:
            rows = dst.shape[0]
            for r0 in range(0, rows, 2048):
                r1 = min(rows, r0 + 2048)
                P.dma("pool", dst[r0:r1, :], src[r0:r1, :])
        cb = [C.sb([128, 2048], F32) for _ in range(2)]
        eb = [C.sb([128, 2048], BF16) for _ in range(2)]
        Rcb = [Res(), Res()]
        Reb = [Res(), Res()]
        for i in range(4):
            b = i % 2
            P.dma("sp", cb[b][:], Dm["cbm"][:, i * 2048:(i + 1) * 2048], w=[Rcb[b]])
            P.act(eb[b][:], cb[b][:], AF.Exp, r=[Rcb[b]], w=[Reb[b]])
            P.dma("sp", Dm["eb_d"][:, i * 2048:(i + 1) * 2048], eb[b][:], r=[Reb[b]])
        cT = C.sb([128, KD, 2], F32)
        cact = C.sb([128, KD, 2], F32)
        ones = C.sb([1, 2], F32)
        RcT, Rcact, Rones = Res(), Res(), Res()
        P.dma("sp", cT[:], Dm["cT"], w=[RcT])
        P.memset("pool", ones[:], 1.0, w=[Rones])
        P.act(cact[:], cT[:], AF.Silu, r=[RcT], w=[Rcact])
        wb = [C.sb([128, KD, 512], F32) for _ in range(2)]
        bb = [C.sb([1, 512], F32) for _ in range(2)]
        pm = [C.ps() for _ in range(2)]
        mo = [C.sb([2, 512], F32) for _ in range(2)]
        Rwb, Rbb, Rpm, Rmo = [Res(), Res()], [Res(), Res()], [Res(), Res()], [Res(), Res()]
        for l in range(2):
            wv = Dm["ada_w"][l].rearrange("(k p) n -> p k n", p=128)
            for n in range(12):
                b = (l * 12 + n) % 2
                P.dma("sp", wb[b][:], wv[:, :, n * 512:(n + 1) * 512], w=[Rwb[b]])
                P.dma("sp", bb[b][:], Dm["ada_b"][l:l + 1, n * 512:(n + 1) * 512], w=[Rbb[b]])
                for k in range(KD):
                    P.mm(pm[b][0:2, :], cact[:, k, :], wb[b][:, k, :], start=(k == 0), stop=False,
                         r=[Rcact, Rwb[b]], w=[Rpm[b]])
                P.mm(pm[b][0:2, :], ones[0:1, :], bb[b][0:1, :], start=False, stop=True,
                     r=[Rones, Rbb[b]], w=[Rpm[b]])
                P.cp("dve", mo[b][:], pm[b][0:2, :], r=[Rpm[b]], w=[Rmo[b]])
                P.dma("sp", Dm["mod_d"][l, :, n * 512:(n + 1) * 512], mo[b][:], r=[Rmo[b]])
        P.emit()


def mla_phaseA(nc, Dm, S, si, x_in, pers):
    nt = S // 128
    cqT, ckvT, KT = pers
    with contextlib.ExitStack() as es:
        C = Ctx(nc, es)
        P = Prog(nc)
        ident = C.sb([128, 128], BF16)
        Rid = Res()
        P.dma("sp", ident[:], Dm["ident"], w=[Rid])
        wd = C.sb([128, KD, 448], BF16)
        Rwd = Res()
        P.dma("sp", wd[:], Dm["wdkv_bf"].rearrange("(k p) n -> p k n", p=128), w=[Rwd])
        g, Rg = load_gain(P, C, Dm["norm_pre_mix"][0:1, :])
        sc, Rsc = load_gain(P, C, Dm["mod_d"][0, si:si + 1, 1 * D:2 * D])
        sh, Rsh = load_gain(P, C, Dm["mod_d"][0, si:si + 1, 0:D])
        gq, Rgq = load_gain(P, C, Dm["mla_q_norm"][0:1, :], QL)
        gkv, Rgkv = load_gain(P, C, Dm["mla_kv_norm"][0:1, :], KVL)
        G1 = C.sb([128, D], F32)
        RG1 = Res()
        P.stt(G1[:], sc[:], 1.0, g[:], ALU.add, ALU.mult, r=[Rsc, Rg], w=[RG1])
        ck = C.sb([128, nt, 32], F32)
        sk = C.sb([128, nt, 32], F32)
        Rck, Rsk = Res(), Res()
        P.dma("sp", ck[:], Dm["ck%d" % si], w=[Rck])
        P.dma("sp", sk[:], Dm["sk%d" % si], w=[Rsk])
        nh = C.sb([128, 2], F32)
        Rnh = Res()
        P.memset("pool", nh[:], -0.5, w=[Rnh])
        junk = C.sb([128, D], BF16)
        Rjunk = Res()
        xt = [C.sb([128, D], F32) for _ in range(2)]
        tmp = [C.sb([128, D], F32) for _ in range(2)]
        hb = [C.sb([128, D], BF16) for _ in range(2)]
        hT = [C.sb([128, KD, 128], BF16) for _ in range(2)]
        ltb = [C.sb([128, 416], BF16) for _ in range(2)]
        t1 = [C.sb([128, 32], F32) for _ in range(2)]
        t2 = [C.sb([128, 32], F32) for _ in range(2)]
        ss = [C.sb([128, 4], F32) for _ in range(2)]
        vv = [C.sb([128, 4], F32) for _ in range(2)]
        rs = [C.sb([128, 4], F32) for _ in range(2)]
        psT = [C.ps(bf=True) for _ in range(2)]
        psL = [C.ps() for _ in range(2)]
        psT2 = [C.ps(bf=True) for _ in range(2)]
        mk = lambda: [Res(), Res()]
        Rx, Rtmp, Rhb, RhT, Rltb, Rt1, Rt2, Rss, Rvv, Rrs, RpsT, RpsL, RpsT2 = [mk() for _ in range(13)]
        Rss2, Rvv2, Rrs2 = mk(), mk(), mk()
        def A1(t):
            b = t % 2
            tok = slice(t * 128, (t + 1) * 128)
            P.dma("sp", xt[b][:], x_in[tok, :], w=[Rx[b]])
            P.act(junk[:], xt[b][:], AF.Square, r=[Rx[b]], w=[Rjunk, Rss[b]], accum_out=ss[b][:, 0:1])
            rstd_ops(P, ss[b][:, 0:1], vv[b][:, 0:1], rs[b][:, 0:1], nh[:, 0:1], 1.0 / D, Rss[b], Rvv[b], Rrs[b], Rnh)
            P.stt(tmp[b][:], xt[b][:], rs[b][:, 0:1], G1[:], ALU.mult, ALU.mult, r=[Rx[b], Rrs[b], RG1], w=[Rtmp[b]])
            P.tt("pool", hb[b][:], tmp[b][:], sh[:], ALU.add, r=[Rtmp[b], Rsh], w=[Rhb[b]])

        def A2a(t):
            b = t % 2
            pT3 = psT[b][:].rearrange("p (k c) -> p k c", c=128)
            for k in range(KD):
                P.tr(pT3[:, k, :], hb[b][:, k * 128:(k + 1) * 128], ident[:], r=[Rhb[b], Rid], w=[RpsT[b]])
            P.cp("act", hT[b][:], pT3, r=[RpsT[b]], w=[RhT[b]])

        def A2b(t):
            b = t % 2
            for k in range(KD):
                P.mm(psL[b][:, 0:448], hT[b][:, k, :], wd[:, k, :], start=(k == 0), stop=(k == KD - 1),
                     r=[RhT[b], Rwd], w=[RpsL[b]])

        def A3(t):
            b = t % 2
            P.act(junk[:, 0:QL], psL[b][:, 0:QL], AF.Square, r=[RpsL[b]], w=[Rjunk, Rss2[b]], accum_out=ss[b][:, 1:2])
            P.act(junk[:, 0:KVL], psL[b][:, QL:QL + KVL], AF.Square, r=[RpsL[b]], w=[Rjunk, Rss2[b]],
                  accum_out=ss[b][:, 2:3])
            P.ts("pool", vv[b][:, 1:2], ss[b][:, 1:2], 1.0 / QL, EPS, ALU.mult, ALU.add, r=[Rss2[b]], w=[Rvv2[b]])
            P.ts("pool", vv[b][:, 2:3], ss[b][:, 2:3], 1.0 / KVL, EPS, ALU.mult, ALU.add, r=[Rss2[b]], w=[Rvv2[b]])
            P.tt("pool", rs[b][:, 1:3], vv[b][:, 1:3], nh[:, 0:2], ALU.pow, r=[Rvv2[b], Rnh], w=[Rrs2[b]])
            P.stt(ltb[b][:, 0:QL], psL[b][:, 0:QL], rs[b][:, 1:2], gq[:], ALU.mult, ALU.mult,
                  r=[RpsL[b], Rrs2[b], Rgq], w=[Rltb[b]])
            P.stt(ltb[b][:, QL:QL + KVL], psL[b][:, QL:QL + KVL], rs[b][:, 2:3], gkv[:], ALU.mult, ALU.mult,
                  r=[RpsL[b], Rrs2[b], Rgkv], w=[Rltb[b]])
            P.tt("dve", t1[b][:], psL[b][:, 384:416], ck[:, t, :], ALU.mult, r=[RpsL[b], Rck], w=[Rt1[b]])
            P.tt("dve", t2[b][:], psL[b][:, 416:448], sk[:, t, :], ALU.mult, r=[RpsL[b], Rsk], w=[Rt2[b]])
            P.tt("pool", ltb[b][:, 384:416], t1[b][:], t2[b][:], ALU.add, r=[Rt1[b], Rt2[b]], w=[Rltb[b]])

        def A4a(t):
            b = t % 2
            q3 = psT2[b][:].rearrange("p (k c) -> p k c", c=128)
            for k in range(3):
                P.tr(q3[:, k, :], ltb[b][:, k * 128:(k + 1) * 128], ident[:], r=[Rltb[b], Rid], w=[RpsT2[b]])
            P.tr(q3[:, 3, :], ltb[b][:, 288:416], ident[:], r=[Rltb[b], Rid], w=[RpsT2[b]])

        def A4b(t):
            b = t % 2
            tok = slice(t * 128, (t + 1) * 128)
            q3 = psT2[b][:].rearrange("p (k c) -> p k c", c=128)
            P.cp("dve", cqT[:, :, tok], q3[:, 0:2, :], r=[RpsT2[b]], w=[Res()])
            P.cp("dve", ckvT[:, tok], q3[:, 2, :], r=[RpsT2[b]], w=[Res()])
            P.cp("dve", KT[0][64:96, tok], q3[96:128, 3, :], r=[RpsT2[b]], w=[Res()])
            P.cp("dve", KT[1][64:96, tok], q3[96:128, 3, :], r=[RpsT2[b]], w=[Res()])

        A1(0)
        for t in range(nt + 1):
            if t < nt:
                A2a(t)
            if t >= 1:
                A4a(t - 1)
            if t < nt:
                A2b(t)
            if t + 1 < nt:
                A1(t + 1)
            if t >= 1:
                A4b(t - 1)
            if t < nt:
                A3(t)
        P.emit()


def mla_phaseB(nc, Dm, S, si, pers):
    nt = S // 128
    QB = min(1024, S)
    NQ = QB // 512
    nqb = S // QB
    NSB = 3
    cqT, ckvT, KT = pers
    with contextlib.ExitStack() as es:
        C = Ctx(nc, es)
        P = Prog(nc)
        wuq = C.sb([128, 2, H * 192], BF16)
        wukv = C.sb([128, H * 128], BF16)
        Rwuq, Rwukv = Res(), Res()
        P.dma("sp", wuq[:], Dm["wuq_bf"].rearrange("(k p) n -> p k n", p=128), w=[Rwuq])
        P.dma("sp", wukv[:], Dm["wukv_bf"], w=[Rwukv])
        V = [C.sb([128, nt, 128], BF16) for _ in range(2)]
        RV = [Res(), Res()]
        for b in range(2):
            P.memset("pool", V[b][:], 1.0, w=[RV[b]])
        RK = [Res(), Res()]
        QT = [C.sb([96, QB], BF16) for _ in range(2)]
        ctab = [C.sb([96, QB], F32) for _ in range(2)]
        stab = [C.sb([96, QB], F32) for _ in range(2)]
        RQ, Rtab = [Res(), Res()], [Res(), Res()]
        t1 = C.sb([96, 512], F32)
        t2 = C.sb([96, 512], F32)
        Rt1, Rt2 = Res(), Res()
        PT = [C.sb([128, QB], BF16) for _ in range(3)]
        RPT = [Res(), Res(), Res()]
        psS = [C.ps2() for _ in range(NSB)]
        RpsS = [Res() for _ in range(NSB)]
        psO = C.ps2()
        RpsO = Res()
        Osb = C.sb([128, QB], F32)
        rsum = C.sb([64, QB], F32)
        OTn = [C.sb([64, QB], BF16) for _ in range(2)]
        ROsb, Rrsum, ROTn = Res(), Res(), [Res(), Res()]
        sc = [0]

        def new_slot():
            k = sc[0] % NSB
            sc[0] += 1
            return k

        def chunk_K(h, c0):
            def emit():
                k = new_slot()
                kb = h % 2
                for c in range(c0, min(c0 + 2, S // 512)):
                    bank = psS[k][:, (c - c0) * 512:(c - c0 + 1) * 512]
                    cs = slice(c * 512, (c + 1) * 512)
                    P.mm(bank, wukv[:, h * 128:(h + 1) * 128], ckvT[:, cs], r=[Rwukv], w=[RpsS[k]])
                    P.cp("dve", KT[kb][0:64, cs], bank[0:64, :], r=[RpsS[k]], w=[RK[kb]])
            return emit

        gsz = min(8, nt)

        def chunk_V(h, g0):
            def emit():
                k = new_slot()
                kb = h % 2
                for gi, g in enumerate(range(g0, min(g0 + 2 * gsz, nt), gsz)):
                    bank = psS[k][:, gi * 512:(gi + 1) * 512]
                    for j in range(gsz):
                        kt = g + j
                        P.mm(bank[:, j * 64:(j + 1) * 64], ckvT[:, kt * 128:(kt + 1) * 128],
                             wukv[:, h * 128 + 64:h * 128 + 128], r=[Rwukv], w=[RpsS[k]])
                    P.cp("dve", V[kb][:, g:g + gsz, 0:64], bank[:, 0:gsz * 64].rearrange("p (j d) -> p j d", d=64),
                         r=[RpsS[k]], w=[RV[kb]])
            return emit

        def chunk_Q(h, qb, qf, j):
            def emit():
                k = new_slot()
                pa, pb, Rk = psS[k][:, 0:512], psS[k][:, 512:1024], RpsS[k]
                if j == 0:
                    qs = slice(qb * QB, (qb + 1) * QB)
                    P.dma("sp", ctab[qf][64:96, :], Dm["cq%d" % si][:, qs], w=[Rtab[qf]])
                    P.dma("sp", stab[qf][64:96, :], Dm["sq%d" % si][:, qs], w=[Rtab[qf]])
                cs = slice(qb * QB + j * 512, qb * QB + (j + 1) * 512)
                js = slice(j * 512, (j + 1) * 512)
                for kk in range(2):
                    P.mm(pa[0:96, :], wuq[:, kk, h * 192:h * 192 + 96], cqT[:, kk, cs], start=(kk == 0), stop=(kk == 1),
                         r=[Rwuq], w=[Rk])
                for kk in range(2):
                    P.mm(pb[0:96, :], wuq[:, kk, h * 192 + 96:h * 192 + 192], cqT[:, kk, cs], start=(kk == 0),
                         stop=(kk == 1), r=[Rwuq], w=[Rk])
                P.ts("dve", QT[qf][0:64, js], pa[0:64, :], MLA_SCALE, None, ALU.mult, r=[Rk], w=[RQ[qf]])
                P.tt("dve", t1[64:96, :], pa[64:96, :], ctab[qf][64:96, js], ALU.mult, r=[Rk, Rtab[qf]], w=[Rt1])
                P.tt("dve", t2[64:96, :], pb[64:96, :], stab[qf][64:96, js], ALU.mult, r=[Rk, Rtab[qf]], w=[Rt2])
                P.tt("pool", QT[qf][64:96, js], t1[64:96, :], t2[64:96, :], ALU.add, r=[Rt1, Rt2], w=[RQ[qf]])
            return emit

        def kv_chunks(h):
            return [chunk_K(h, c0) for c0 in range(0, S // 512, 2)] + [chunk_V(h, g0) for g0 in range(0, nt, 2 * gsz)]

        def q_chunks(h, qb, qf):
            return [chunk_Q(h, qb, qf, j) for j in range(NQ)]

        def attend(h, qb, qf, ob, todo):
            kb = h % 2
            slot_of = {}
            points = {}
            for ch in todo:
                ch()

            def QK(kt):
                k = new_slot()
                slot_of[kt] = k
                for j in range(NQ):
                    js = slice(j * 512, (j + 1) * 512)
                    P.mm(psS[k][:, js], KT[kb][0:96, kt * 128:(kt + 1) * 128], QT[qf][0:96, js],
                         r=[RK[kb], RQ[qf]], w=[RpsS[k]])

            for kt in range(min(NSB, nt)):
                QK(kt)
            for kt in range(nt):
                k = slot_of[kt]
                P.act(PT[kt % 3][:], psS[k][:, 0:QB], AF.Exp, r=[RpsS[k]], w=[RPT[kt % 3]])
                for j in range(NQ):
                    js = slice(j * 512, (j + 1) * 512)
                    P.mm(psO[:, js], V[kb][:, kt, :], PT[kt % 3][:, js], start=(kt == 0), stop=(kt == nt - 1),
                         r=[RV[kb], RPT[kt % 3]], w=[RpsO])
                if kt + NSB < nt:
                    QK(kt + NSB)
                for ch in points.get(kt, []):
                    ch()
            P.cp("dve", Osb[:], psO[:, 0:QB], r=[RpsO], w=[ROsb])
            P.recip(rsum[0:64, :], Osb[64:128, :], r=[ROsb], w=[Rrsum])
            P.tt("dve", OTn[ob][:], Osb[0:64, :], rsum[0:64, :], ALU.mult, r=[ROsb, Rrsum], w=[ROTn[ob]])
            P.dma("sp", Dm["ot_d"][h * 64:(h + 1) * 64, qb * QB:(qb + 1) * QB], OTn[ob][:], r=[ROTn[ob]])

        items = [(h, qb) for h in range(H) for qb in range(nqb)]
        for ch in kv_chunks(0) + q_chunks(0, 0, 0):
            ch()
        kvq = []
        for i, (h, qb) in enumerate(items):
            if qb == 0 and h + 1 < H:
                kvq = kv_chunks(h + 1)
            nkv = -(-len(kvq) // (nqb - qb))
            todo, kvq = kvq[:nkv], kvq[nkv:]
            if i + 1 < len(items):
                todo = todo + q_chunks(items[i + 1][0], items[i + 1][1], (i + 1) % 2)
            attend(h, qb, i % 2, i % 2, todo)
        P.emit()


def mla_phaseC(nc, Dm, S, si, x_in, x_out):
    nt = S // 128
    TB = min(512, S)
    with contextlib.ExitStack() as es:
        C = Ctx(nc, es)
        P = Prog(nc)
        wo = C.sb([128, KD, D], BF16)
        Rwo = Res()
        P.dma("sp", wo[:], Dm["wo_bf"].rearrange("(k p) n -> p k n", p=128), w=[Rwo])
        g, Rg = load_gain(P, C, Dm["norm_post_mix"][0:1, :])
        gt, Rgt = load_gain(P, C, Dm["mod_d"][0, si:si + 1, 2 * D:3 * D])
        G2 = C.sb([128, D], F32)
        RG2 = Res()
        P.tt("dve", G2[:], g[:], gt[:], ALU.mult, r=[Rg, Rgt], w=[RG2])
        nh = C.sb([128, 1], F32)
        Rnh = Res()
        P.memset("pool", nh[:], -0.5, w=[Rnh])
        junk = C.sb([128, D], BF16)
        Rjunk = Res()
        otv = Dm["ot_d"].rearrange("(k p) s -> p k s", p=128)
        OTt = [C.sb([128, KD, TB], BF16) for _ in range(2)]
        xt = [C.sb([128, D], F32) for _ in range(2)]
        tmp = [C.sb([128, D], F32) for _ in range(2)]
        ss = [C.sb([128, 1], F32) for _ in range(2)]
        vv = [C.sb([128, 1], F32) for _ in range(2)]
        rs = [C.sb([128, 1], F32) for _ in range(2)]
        psY = [C.ps2() for _ in range(2)]
        mk = lambda: [Res(), Res()]
        ROT, Rx, Rtmp, Rss, Rvv, Rrs, RpsY = [mk() for _ in range(7)]
        for t in range(nt):
            b = t % 2
            blk = (t * 128) // TB
            bb = blk % 2
            if (t * 128) % TB == 0:
                P.dma("sp", OTt[bb][:], otv[:, :, blk * TB:(blk + 1) * TB], w=[ROT[bb]])
            o0 = (t * 128) % TB
            tok = slice(t * 128, (t + 1) * 128)
            P.dma("sp", xt[b][:], x_in[tok, :], w=[Rx[b]])
            for n in range(2):
                for k in range(KD):
                    P.mm(psY[b][:, n * 512:(n + 1) * 512], OTt[bb][:, k, o0:o0 + 128], wo[:, k, n * 512:(n + 1) * 512],
                         start=(k == 0), stop=(k == KD - 1), r=[ROT[bb], Rwo], w=[RpsY[b]])
            P.act(junk[:], psY[b][:], AF.Square, r=[RpsY[b]], w=[Rjunk, Rss[b]], accum_out=ss[b][:, 0:1])
            rstd_ops(P, ss[b][:], vv[b][:], rs[b][:], nh[:], 1.0 / D, Rss[b], Rvv[b], Rrs[b], Rnh)
            P.stt(tmp[b][:], psY[b][:], rs[b][:, 0:1], G2[:], ALU.mult, ALU.mult, r=[RpsY[b], Rrs[b], RG2], w=[Rtmp[b]])
            P.tt("pool", xt[b][:], tmp[b][:], xt[b][:], ALU.add, r=[Rtmp[b], Rx[b]], w=[Rx[b]])
            P.dma("sp", x_out[tok, :], xt[b][:], r=[Rx[b]])
        P.emit()


def ffn_phase(nc, Dm, S, si, l, x_in, x_out):
    TB = 256
    nb = S // TB
    with contextlib.ExitStack() as es:
        C = Ctx(nc, es)
        P = Prog(nc)
        ident = C.sb([128, 128], BF16)
        Rid = Res()
        P.dma("sp", ident[:], Dm["ident"], w=[Rid])
        wgu = C.sb([128, KD, 2 * FH], BF16)
        wdn = C.sb([128, NJ, D], BF16)
        Rwgu, Rwdn = Res(), Res()
        gv = Dm["wgu_bf"][l].rearrange("(k p) n -> p k n", p=128)
        for k in range(KD):
            P.dma("sp", wgu[:, k, :], gv[:, k, :], w=[Rwgu])
        dv = Dm["wdn_bf"][l].rearrange("(j p) n -> p j n", p=128)
        for j0 in range(0, NJ, 11):
            P.dma("sp", wdn[:, j0:j0 + 11, :], dv[:, j0:j0 + 11, :], w=[Rwdn])
        G3 = C.sb([128, D], F32)
        G4 = C.sb([128, D], F32)
        RG3, RG4 = Res(), Res()
        sh, Rsh = load_gain(P, C, Dm["mod_d"][l, si:si + 1, 3 * D:4 * D])
        nh = C.sb([128, 1], F32)
        Rnh = Res()
        P.memset("pool", nh[:], -0.5, w=[Rnh])
        junk = C.sb([128, D], BF16)
        Rjunk = Res()
        xres = [C.sb([128, 2, D], F32) for _ in range(2)]
        tmp = [C.sb([128, D], F32) for _ in range(2)]
        hb = [C.sb([128, D], BF16) for _ in range(2)]
        hT = [C.sb([128, KD, TB], BF16) for _ in range(2)]
        actT = C.sb([128, NJ, TB], BF16)
        sg = [C.sb([128, TB], F32) for _ in range(2)]
        ss = [C.sb([128, 2], F32) for _ in range(4)]
        vv = [C.sb([128, 2], F32) for _ in range(4)]
        rs = [C.sb([128, 2], F32) for _ in range(4)]
        psT = [C.ps(bf=True) for _ in range(2)]
        psGU = [C.ps() for _ in range(2)]
        psY = [C.ps2() for _ in range(2)]
        mk = lambda n=2: [Res() for _ in range(n)]
        Rx = [mk(), mk()]
        Rtmp, Rhb, RhT, Rsg, RpsT, RpsGU, RpsY = [mk() for _ in range(7)]
        Rss, Rvv, Rrs, Rss2, Rvv2, Rrs2 = [mk(4) for _ in range(6)]
        RactT = Res()
        tmp2 = [C.sb([128, D], F32) for _ in range(2)]
        Rtmp2 = mk()
        P.dma("sp", G3[:], Dm["norm_pre_ffn"][l:l + 1, :].broadcast_to([128, D]), w=[RG3])
        P.dma("sp", tmp[0][:], Dm["mod_d"][l, si:si + 1, 4 * D:5 * D].broadcast_to([128, D]), w=[Rtmp[0]])
        P.stt(G3[:], tmp[0][:], 1.0, G3[:], ALU.add, ALU.mult, r=[Rtmp[0], RG3], w=[RG3])
        P.dma("sp", G4[:], Dm["norm_post_ffn"][l:l + 1, :].broadcast_to([128, D]), w=[RG4])
        P.dma("sp", tmp[1][:], Dm["mod_d"][l, si:si + 1, 5 * D:6 * D].broadcast_to([128, D]), w=[Rtmp[1]])
        P.tt("dve", G4[:], G4[:], tmp[1][:], ALU.mult, r=[RG4, Rtmp[1]], w=[RG4])

        def pre_a(b):
            bb = b % 2
            for i in range(2):
                t = 2 * b + i
                q = t % 4
                tok = slice(t * 128, (t + 1) * 128)
                P.dma("sp", xres[bb][:, i, :], x_in[tok, :], w=[Rx[bb][i]])
                P.act(junk[:], xres[bb][:, i, :], AF.Square, r=[Rx[bb][i]], w=[Rjunk, Rss[q]], accum_out=ss[q][:, 0:1])
                rstd_ops(P, ss[q][:, 0:1], vv[q][:, 0:1], rs[q][:, 0:1], nh[:], 1.0 / D, Rss[q], Rvv[q], Rrs[q], Rnh)
                P.stt(tmp[i][:], xres[bb][:, i, :], rs[q][:, 0:1], G3[:], ALU.mult, ALU.mult,
                      r=[Rx[bb][i], Rrs[q], RG3], w=[Rtmp[i]])
                P.tt("pool", hb[i][:], tmp[i][:], sh[:], ALU.add, r=[Rtmp[i], Rsh], w=[Rhb[i]])

        def pre_b(b):
            bb = b % 2
            for i in range(2):
                pT3 = psT[i][:].rearrange("p (k c) -> p k c", c=128)
                for k in range(KD):
                    P.tr(pT3[:, k, :], hb[i][:, k * 128:(k + 1) * 128], ident[:], r=[Rhb[i], Rid], w=[RpsT[i]])
                P.cp("dve", hT[bb][:, :, i * 128:(i + 1) * 128], pT3, r=[RpsT[i]], w=[RhT[bb]])

        def gu(b):
            bb = b % 2
            for j in range(NJ):
                p = j % 2
                for k in range(KD):
                    P.mm(psGU[p][:, 0:TB], wgu[:, k, j * 128:(j + 1) * 128], hT[bb][:, k, :], start=(k == 0),
                         stop=(k == KD - 1), r=[Rwgu, RhT[bb]], w=[RpsGU[p]])
                for k in range(KD):
                    P.mm(psGU[p][:, TB:2 * TB], wgu[:, k, FH + j * 128:FH + (j + 1) * 128], hT[bb][:, k, :],
                         start=(k == 0), stop=(k == KD - 1), r=[Rwgu, RhT[bb]], w=[RpsGU[p]])
                P.act(sg[p][:], psGU[p][:, 0:TB], AF.Silu, r=[RpsGU[p]], w=[Rsg[p]])
                P.tt("dve", actT[:, j, :], sg[p][:], psGU[p][:, TB:2 * TB], ALU.mult, r=[Rsg[p], RpsGU[p]], w=[RactT])

        def down_post(b, i):
            bb = b % 2
            t = 2 * b + i
            q = t % 4
            tok = slice(t * 128, (t + 1) * 128)
            for n in range(2):
                for j in range(NJ):
                    P.mm(psY[i][:, n * 512:(n + 1) * 512], actT[:, j, i * 128:(i + 1) * 128],
                         wdn[:, j, n * 512:(n + 1) * 512], start=(j == 0), stop=(j == NJ - 1),
                         r=[RactT, Rwdn], w=[RpsY[i]])
            P.act(junk[:], psY[i][:], AF.Square, r=[RpsY[i]], w=[Rjunk, Rss2[q]], accum_out=ss[q][:, 1:2])
            rstd_ops(P, ss[q][:, 1:2], vv[q][:, 1:2], rs[q][:, 1:2], nh[:], 1.0 / D, Rss2[q], Rvv2[q], Rrs2[q], Rnh)
            P.stt(tmp2[i][:], psY[i][:], rs[q][:, 1:2], G4[:], ALU.mult, ALU.mult, r=[RpsY[i], Rrs2[q], RG4], w=[Rtmp2[i]])
            P.tt("pool", xres[bb][:, i, :], tmp2[i][:], xres[bb][:, i, :], ALU.add, r=[Rtmp2[i], Rx[bb][i]], w=[Rx[bb][i]])
            P.dma("sp", x_out[tok, :], xres[bb][:, i, :], r=[Rx[bb][i]])

        pre_a(0)
        pre_b(0)
        for b in range(nb):
            gu(b)
            if b + 1 < nb:
                pre_a(b + 1)
            down_post(b, 0)
            if b + 1 < nb:
                pre_b(b + 1)
            down_post(b, 1)
        P.emit()


def na_plan(S):
    rows = S // GRID_W
    nt = rows // 2
    rstart = lambda r: min(max(r - WIN_R // 2, 0), rows - WIN_R)
    plan = []
    for i in range(nt):
        lo = min(rstart(2 * i), rstart(2 * i + 1))
        hi = max(rstart(2 * i), rstart(2 * i + 1)) + WIN_R - 1
        kts = []
        for kt in range(lo // 2, hi // 2 + 1):
            key = []
            for jr in range(2):
                for qr in range(2):
                    kr, q = 2 * kt + jr, 2 * i + qr
                    ok = rstart(q) <= kr < rstart(q) + WIN_R
                    key.append(kr - q + WIN_R - 1 if ok else 15)
            kts.append((kt, tuple(key)))
        plan.append(kts)
    return plan


def na_phase(nc, Dm, S, si, x_in, x_out):
    nt = S // 128
    LAG = 3
    NSL = 8
    plan = na_plan(S)
    interior = [tuple(2 * off + jr - qr + 7 if -4 <= 2 * off + jr - qr <= 3 else 15 for jr in range(2) for qr in range(2))
                for off in range(-2, 3)]
    with contextlib.ExitStack() as es:
        C = Ctx(nc, es)
        P = Prog(nc)
        ident = C.sb([128, 128], BF16)
        Rid = Res()
        P.dma("sp", ident[:], Dm["ident"], w=[Rid])
        wq = C.sb([128, KD, 3 * D], BF16)
        wo = C.sb([128, KD, D], BF16)
        Rwq, Rwo = Res(), Res()
        qv = Dm["nqkv_bf"].rearrange("(k p) n -> p k n", p=128)
        for k in range(KD):
            P.dma("sp", wq[:, k, :], qv[:, k, :], w=[Rwq])
        P.dma("sp", wo[:], Dm["nwo_bf"].rearrange("(k p) n -> p k n", p=128), w=[Rwo])
        G5 = C.sb([128, D], F32)
        G6 = C.sb([128, D], F32)
        RG5, RG6 = Res(), Res()
        sh, Rsh = load_gain(P, C, Dm["mod_d"][1, si:si + 1, 0:D])
        nh = C.sb([128, 1], F32)
        Rnh = Res()
        P.memset("pool", nh[:], -0.5, w=[Rnh])
        junk = C.sb([128, D], BF16)
        Rjunk = Res()
        NPAT = 9
        pat = [C.sb([128, 2, H, 64], BF16) for _ in range(NPAT)]
        Rpat = [Res() for _ in range(NPAT)]
        ebv = Dm["eb_d"].rearrange("(e a) (b h q) -> e (a b) h q", e=16, h=H, q=64)

        def load_pat(slot, key):
            for jr in range(2):
                for qr in range(2):
                    P.dma("sp", pat[slot][jr * 64:(jr + 1) * 64, qr, :, :], ebv[key[jr * 2 + qr]], w=[Rpat[slot]])

        slot_of = {}
        for s_, key in enumerate(interior):
            load_pat(s_, key)
            slot_of[key] = s_
        rot = [0]

        def get_pat(key):
            if key in slot_of:
                return slot_of[key]
            s_ = 5 + rot[0] % (NPAT - 5)
            rot[0] += 1
            load_pat(s_, key)
            return s_

        KTr = C.sb([128, KD, NSL, 128], BF16)
        QT2 = C.sb([128, KD, 4, 2, 128], BF16)
        Vr = C.sb([128, NSL, H, 65], BF16)
        RKs = [Res() for _ in range(NSL)]
        RVs = [Res() for _ in range(NSL)]
        RQs = [Res() for _ in range(4)]
        RVall = Res()
        P.memset("pool", Vr[:], 1.0, w=[RVall] + RVs)
        P.memset("pool", QT2[:], 0.0, w=RQs)
        xt = [C.sb([128, D], F32) for _ in range(2)]
        tmp = [C.sb([128, D], F32) for _ in range(2)]
        hb = [C.sb([128, D], BF16) for _ in range(2)]
        hT = [C.sb([128, KD, 128], BF16) for _ in range(2)]
        exS = [C.sb([128, 512], BF16) for _ in range(2)]
        PTt = [C.sb([128, 512], BF16) for _ in range(3)]
        On = [C.sb([128, D], BF16) for _ in range(2)]
        tmp2 = C.sb([128, D], F32)
        Rtmp2 = Res()
        OT = C.sb([128, KD, 128], BF16)
        xr = C.sb([128, D], F32)
        rsm = C.sb([128, H], F32)
        ss = [C.sb([128, 2], F32) for _ in range(2)]
        vv = [C.sb([128, 2], F32) for _ in range(2)]
        rs = [C.sb([128, 2], F32) for _ in range(2)]
        psT = C.ps(bf=True)
        psQK = C.ps2()
        psV = C.ps2()
        psS = [C.ps(), C.ps()]
        psO3 = C.ps()
        RpsT, RpsQK, RpsV, RpsO3 = Res(), Res(), Res(), Res()
        RpsS = [Res(), Res()]
        mk = lambda: [Res(), Res()]
        Rx, Rtmp, RhT, RexS, Rss, Rvv, Rrs, Rss2, Rvv2, Rrs2 = [mk() for _ in range(10)]
        ROT, Rxr, Rrsm = Res(), Res(), Res()
        Rhb = [Res(), Res()]
        ROn = [Res(), Res()]
        RPT = [Res(), Res(), Res()]

        P.dma("sp", G5[:], Dm["norm_pre_mix"][1:2, :].broadcast_to([128, D]), w=[RG5])
        P.dma("sp", tmp[0][:], Dm["mod_d"][1, si:si + 1, 1 * D:2 * D].broadcast_to([128, D]), w=[Rtmp[0]])
        P.stt(G5[:], tmp[0][:], 1.0, G5[:], ALU.add, ALU.mult, r=[Rtmp[0], RG5], w=[RG5])
        P.dma("sp", G6[:], Dm["norm_post_mix"][1:2, :].broadcast_to([128, D]), w=[RG6])
        P.dma("sp", tmp[1][:], Dm["mod_d"][1, si:si + 1, 2 * D:3 * D].broadcast_to([128, D]), w=[Rtmp[1]])
        P.tt("dve", G6[:], G6[:], tmp[1][:], ALU.mult, r=[RG6, Rtmp[1]], w=[RG6])

        def obank(h):
            if h < 7:
                return psV[:, h * 65:(h + 1) * 65], RpsV
            if h < 14:
                return psV[:, 512 + (h - 7) * 65:512 + (h - 6) * 65], RpsV
            return psO3[:, (h - 14) * 65:(h - 13) * 65], RpsO3

        def pre(t):
            b = t % 2
            tok = slice(t * 128, (t + 1) * 128)
            P.dma("sp", xt[b][:], x_in[tok, :], w=[Rx[b]])
            P.act(junk[:], xt[b][:], AF.Square, r=[Rx[b]], w=[Rjunk, Rss[b]], accum_out=ss[b][:, 0:1])
            rstd_ops(P, ss[b][:, 0:1], vv[b][:, 0:1], rs[b][:, 0:1], nh[:], 1.0 / D, Rss[b], Rvv[b], Rrs[b], Rnh)
            P.stt(tmp[b][:], xt[b][:], rs[b][:, 0:1], G5[:], ALU.mult, ALU.mult, r=[Rx[b], Rrs[b], RG5], w=[Rtmp[b]])
            P.tt("pool", hb[b][:], tmp[b][:], sh[:], ALU.add, r=[Rtmp[b], Rsh], w=[Rhb[b]])

        def stage1(t):
            b = t % 2
            sl = t % NSL
            qsl = t % 4
            pT3 = psT[:].rearrange("p (k c) -> p k c", c=128)
            for k in range(KD):
                P.tr(pT3[:, k, :], hb[b][:, k * 128:(k + 1) * 128], ident[:], r=[Rhb[b], Rid], w=[RpsT])
            P.cp("dve", hT[b][:], pT3, r=[RpsT], w=[RhT[b]])
            q4 = psQK[:].rearrange("p (c q) -> p c q", q=128)
            for part in range(2):
                for c in range(KD):
                    for k in range(KD):
                        P.mm(q4[:, c, :], wq[:, k, part * D + c * 128:part * D + (c + 1) * 128], hT[b][:, k, :],
                             start=(k == 0), stop=(k == KD - 1), r=[Rwq, RhT[b]], w=[RpsQK])
                if part == 1:
                    P.cp("dve", KTr[:, :, sl, :], q4, r=[RpsQK], w=[RKs[sl]])
                else:
                    P.ts("dve", QT2[0:64, :, qsl, 0, :], q4[0:64, :, :], NA_SCALE, None, ALU.mult, r=[RpsQK], w=[RQs[qsl]])
                    P.ts("dve", QT2[64:128, :, qsl, 1, :], q4[64:128, :, :], NA_SCALE, None, ALU.mult, r=[RpsQK], w=[RQs[qsl]])
            for n in range(2):
                for k in range(KD):
                    P.mm(psV[:, n * 512:(n + 1) * 512], hT[b][:, k, :], wq[:, k, 2 * D + n * 512:2 * D + (n + 1) * 512],
                         start=(k == 0), stop=(k == KD - 1), r=[RhT[b], Rwq], w=[RpsV])
            P.cp("dve", Vr[:, sl, :, 0:64], psV[:].rearrange("p (h d) -> p h d", d=64), r=[RpsV], w=[RVs[sl]])

        def stage2a(i):
            qsl = i % 4
            kts = plan[i]
            slots = [get_pat(key) for _, key in kts]
            steps = [(g, n_kt) for g in range(4) for n_kt in range(len(kts))]
            ns = len(steps)

            def ST(s_):
                g, n_kt = steps[s_]
                sl = kts[n_kt][0] % NSL
                for pp in range(2):
                    c = 2 * g + pp
                    P.mm(psS[s_ % 2][:, pp * 256:(pp + 1) * 256], KTr[:, c, sl, :],
                         QT2[:, c, qsl, :, :].rearrange("p e q -> p (e q)"), r=[RKs[sl], RQs[qsl]], w=[RpsS[s_ % 2]])

            def EM(s_):
                g, n_kt = steps[s_]
                sb_, pb_ = s_ % 2, s_ % 3
                P.act(exS[sb_][:], psS[sb_][:], AF.Exp, r=[RpsS[sb_]], w=[RexS[sb_]])
                P.tt("pool" if s_ % 4 == 3 else "dve", PTt[pb_][:].rearrange("p (h r c) -> p h r c", h=4, r=2),
                     exS[sb_][:].rearrange("p (h r c) -> p h r c", h=4, r=2),
                     pat[slots[n_kt]][:, :, 4 * g:4 * g + 4, :].rearrange("p r h c -> p h r c"), ALU.mult,
                     r=[RexS[sb_], Rpat[slots[n_kt]]], w=[RPT[pb_]])

            def PV(s_):
                g, n_kt = steps[s_]
                sl = kts[n_kt][0] % NSL
                for hh in range(4):
                    h = 4 * g + hh
                    oap, Ro = obank(h)
                    P.mm(oap, PTt[s_ % 3][:, hh * 128:(hh + 1) * 128], Vr[:, sl, h, :],
                         start=(h in (0, 7, 14) and n_kt == 0), stop=(n_kt == len(kts) - 1),
                         r=[RPT[s_ % 3], RVs[sl]], w=[Ro], skip=True)

            ST(0)
            if ns > 1:
                ST(1)
            for s_ in range(ns):
                EM(s_)
                PV(s_)
                if s_ + 2 < ns:
                    ST(s_ + 2)
            ob = i % 2
            banks = ((psV[:, 0:455], RpsV, 0, 7), (psV[:, 512:967], RpsV, 7, 7), (psO3[:, 0:130], RpsO3, 14, 2))
            rs3 = rsm[:].rearrange("p (h o) -> p h o", o=1)
            On3 = On[ob][:].rearrange("p (h d) -> p h d", d=64)
            for ap, Rb, h0, nh_ in banks:
                a3 = ap.rearrange("p (h e) -> p h e", e=65)
                P.recip(rs3[:, h0:h0 + nh_, :], a3[:, :, 64:65], r=[Rb], w=[Rrsm])
                P.tt("dve", On3[:, h0:h0 + nh_, :], a3[:, :, 0:64], rs3[:, h0:h0 + nh_, :].broadcast_to([128, nh_, 64]),
                     ALU.mult, r=[Rb, Rrsm], w=[ROn[ob]])

        def stage2b(i):
            ob = i % 2
            tok = slice(i * 128, (i + 1) * 128)
            pT3 = psT[:].rearrange("p (k c) -> p k c", c=128)
            for k in range(KD):
                P.tr(pT3[:, k, :], On[ob][:, k * 128:(k + 1) * 128], ident[:], r=[ROn[ob], Rid], w=[RpsT])
            P.cp("dve", OT[:], pT3, r=[RpsT], w=[ROT])
            for n in range(2):
                for k in range(KD):
                    P.mm(psQK[:, n * 512:(n + 1) * 512], OT[:, k, :], wo[:, k, n * 512:(n + 1) * 512], start=(k == 0),
                         stop=(k == KD - 1), r=[ROT, Rwo], w=[RpsQK])
            b = i % 2
            P.dma("sp", xr[:], x_in[tok, :], w=[Rxr])
            P.act(junk[:], psQK[:], AF.Square, r=[RpsQK], w=[Rjunk, Rss2[b]], accum_out=ss[b][:, 1:2])
            rstd_ops(P, ss[b][:, 1:2], vv[b][:, 1:2], rs[b][:, 1:2], nh[:], 1.0 / D, Rss2[b], Rvv2[b], Rrs2[b], Rnh)
            P.stt(tmp2[:], psQK[:], rs[b][:, 1:2], G6[:], ALU.mult, ALU.mult, r=[RpsQK, Rrs2[b], RG6], w=[Rtmp2])
            P.tt("pool", xr[:], tmp2[:], xr[:], ALU.add, r=[Rtmp2, Rxr], w=[Rxr])
            P.dma("sp", x_out[tok, :], xr[:], r=[Rxr])

        pre(0)
        for t in range(nt + LAG + 1):
            if t < nt:
                stage1(t)
            if t + 1 < nt:
                pre(t + 1)
            if LAG <= t < nt + LAG:
                stage2a(t - LAG)
            if t >= LAG + 1:
                stage2b(t - LAG - 1)
        P.emit()


def build(SP, SS):
    nc = bass.Bass("TRN2", target_bir_lowering=False)
    Dm = {}

    def din(name, shape, dt=F32):
        Dm[name] = nc.dram_tensor(name, list(shape), dt, kind="ExternalInput").ap()

    def dscr(name, shape, dt):
        Dm[name] = nc.dram_tensor(name, list(shape), dt, kind="Internal").ap()

    seqs = [(SP, "xp", "yp"), (SS, "xs", "ys")]
    din("xp", [SP, D])
    din("xs", [SS, D])
    Dm["yp"] = nc.dram_tensor("yp", [SP, D], F32, kind="ExternalOutput").ap()
    Dm["ys"] = nc.dram_tensor("ys", [SS, D], F32, kind="ExternalOutput").ap()
    din("cT", [128, KD, 2])
    din("ada_w", [2, D, 6 * D])
    din("ada_b", [2, 6 * D])
    for n in ("norm_pre_mix", "norm_post_mix", "norm_pre_ffn", "norm_post_ffn"):
        din(n, [2, D])
    din("mla_q_norm", [1, QL])
    din("mla_kv_norm", [1, KVL])
    din("ident", [128, 128], BF16)
    din("cbm", [128, 8192])
    wspecs = [("wdkv", [D, 448]), ("wuq", [QL, H * 192]), ("wukv", [KVL, H * 128]), ("wo", [D, D]),
              ("nqkv", [D, 3 * D]), ("nwo", [D, D]), ("wgu", [2, D, 2 * FH]), ("wdn", [2, FH, D])]
    casts = []
    for n, shp in wspecs:
        din(n + "_f", shp)
        dscr(n + "_bf", shp, BF16)
        tot = int(np.prod(shp))
        fl = " ".join("abc"[:len(shp)])
        pat = "%s -> (%s)" % (fl, fl)
        src = Dm[n + "_f"].rearrange(pat).rearrange("(r c) -> r c", c=64 if tot % 512 else 512)
        dst = Dm[n + "_bf"].rearrange(pat).rearrange("(r c) -> r c", c=64 if tot % 512 else 512)
        casts.append((dst, src))
    for si, (S, _, _) in enumerate(seqs):
        nt = S // 128
        din("ck%d" % si, [128, nt, 32])
        din("sk%d" % si, [128, nt, 32])
        din("cq%d" % si, [32, S])
        din("sq%d" % si, [32, S])
    dscr("mod_d", [2, 2, 6 * D], F32)
    dscr("eb_d", [128, 8192], BF16)
    dscr("ot_d", [D, max(SP, SS)], BF16)
    dscr("xa", [max(SP, SS), D], F32)
    dscr("xb", [max(SP, SS), D], F32)

    phase0(nc, Dm, casts)
    for si, (S, xn, yn) in enumerate(seqs):
        x_in, y_out = Dm[xn], Dm[yn]
        xa, xb = Dm["xa"][0:S, :], Dm["xb"][0:S, :]
        saved_ot = Dm["ot_d"]
        Dm["ot_d"] = saved_ot[:, 0:S]
        with contextlib.ExitStack() as es:
            cqT = es.enter_context(nc.sbuf_tensor("cqT%d" % si, [128, 2, S], BF16))
            ckvT = es.enter_context(nc.sbuf_tensor("ckvT%d" % si, [128, S], BF16))
            KT = [es.enter_context(nc.sbuf_tensor("KT%d_%d" % (si, b), [96, S], BF16)) for b in range(2)]
            pers = (cqT, ckvT, KT)
            mla_phaseA(nc, Dm, S, si, x_in, pers)
            mla_phaseB(nc, Dm, S, si, pers)
        mla_phaseC(nc, Dm, S, si, x_in, xa)
        Dm["ot_d"] = saved_ot
        ffn_phase(nc, Dm, S, si, 0, xa, xb)
        na_phase(nc, Dm, S, si, xb, xa)
        ffn_phase(nc, Dm, S, si, 1, xa, y_out)
    return nc


def _rope_tabs(S):
    inv = (1.0 / (np.float32(10000.0) ** (np.arange(0, 32, 2, dtype=np.float32) / np.float32(32)))).astype(np.float32)
    ang = (np.arange(S, dtype=np.float32)[:, None] * inv[None, :]).astype(np.float32)
    cos, sin = np.cos(ang).astype(np.float32), np.sin(ang).astype(np.float32)
    c2 = np.concatenate([cos, cos], 1)
    s2 = np.concatenate([-sin, sin], 1)
    nt = S // 128
    ck = np.ascontiguousarray(c2.reshape(nt, 128, 32).transpose(1, 0, 2))
    sk = np.ascontiguousarray(s2.reshape(nt, 128, 32).transpose(1, 0, 2))
    cq = np.ascontiguousarray((c2 * np.float32(MLA_SCALE)).T)
    sq = np.ascontiguousarray((s2 * np.float32(MLA_SCALE)).T)
    return ck, sk, cq, sq


_CACHE = {}


def kernel(x_prompt, x_sample, c_prompt, c_sample, ada_w, ada_b, norm_pre_mix, norm_post_mix, norm_pre_ffn,
           norm_post_ffn, mla_w_dkv, mla_q_norm, mla_kv_norm, mla_w_uq, mla_w_ukv, mla_w_o, na_w_qkv, na_rpb, na_w_o,
           ffn_w_gu, ffn_w_down):
    f = lambda a: np.ascontiguousarray(np.asarray(a, dtype=np.float32))
    x_prompt, x_sample = f(x_prompt), f(x_sample)
    SP, SS = x_prompt.shape[1], x_sample.shape[1]
    if (SP, SS) not in _CACHE:
        _CACHE[(SP, SS)] = build(SP, SS)
    nc = _CACHE[(SP, SS)]
    wd = f(mla_w_dkv)[0]
    wdkv = np.concatenate([wd, wd[:, 400:416], wd[:, 384:400]], 1)
    wq = f(mla_w_uq)[0].reshape(QL, H, 96)
    wuq = np.concatenate([wq, wq[:, :, 0:64], wq[:, :, 80:96], wq[:, :, 64:80]], 2).reshape(QL, H * 192)
    rpb = f(na_rpb)[0]
    cidx = np.arange(GRID_W)
    cstart = np.clip(cidx - WIN_C // 2, 0, GRID_W - WIN_C)
    valid = (cidx[None, :] >= cstart[:, None]) & (cidx[None, :] < cstart[:, None] + WIN_C)
    dc = np.clip(cidx[None, :] - cidx[:, None] + WIN_C - 1, 0, 2 * WIN_C - 2)
    cb = rpb[:, :, dc]
    cb = np.where(valid[None, None], cb, np.float32(MASKV))
    cbm = np.full((16, 64, H, 64), MASKV, np.float32)
    cbm[:15] = cb.transpose(1, 3, 0, 2)
    shared = {
        "ada_w": f(ada_w), "ada_b": f(ada_b), "norm_pre_mix": f(norm_pre_mix), "norm_post_mix": f(norm_post_mix),
        "norm_pre_ffn": f(norm_pre_ffn), "norm_post_ffn": f(norm_post_ffn), "mla_q_norm": f(mla_q_norm),
        "mla_kv_norm": f(mla_kv_norm), "ident": np.eye(128, dtype=np.float32).astype(ml_dtypes.bfloat16),
        "cbm": np.ascontiguousarray(cbm.reshape(128, 8192)), "wdkv_f": np.ascontiguousarray(wdkv),
        "wuq_f": np.ascontiguousarray(wuq), "wukv_f": f(mla_w_ukv)[0], "wo_f": f(mla_w_o)[0], "nqkv_f": f(na_w_qkv)[0],
        "nwo_f": f(na_w_o)[0], "wgu_f": f(ffn_w_gu), "wdn_f": f(ffn_w_down),
    }
    for si, S in enumerate((SP, SS)):
        ck, sk, cq, sq = _rope_tabs(S)
        shared.update({"ck%d" % si: ck, "sk%d" % si: sk, "cq%d" % si: cq, "sq%d" % si: sq})
    cp_, cs_ = f(c_prompt), f(c_sample)
    in_maps = []
    for c in range(NCORES):
        m = dict(shared)
        m["xp"] = x_prompt[c]
        m["xs"] = x_sample[c]
        cc = np.stack([cp_[c], cs_[c]], 0)
        m["cT"] = np.ascontiguousarray(cc.reshape(2, KD, 128).transpose(2, 1, 0))
        in_maps.append(m)
    res = run_bass_kernel_spmd(nc, in_maps, core_ids=list(range(NCORES)))
    yp = np.stack([np.asarray(res.results[c]["yp"], dtype=np.float32) for c in range(NCORES)], 0)
    ys = np.stack([np.asarray(res.results[c]["ys"], dtype=np.float32) for c in range(NCORES)], 0)
    return yp, ys
```

```python
import contextlib
import numpy as np
import ml_dtypes
import concourse.bass as bass
import concourse.mybir as mybir
from concourse.bass_utils import run_bass_kernel_spmd

F32 = mybir.dt.float32
BF16 = mybir.dt.bfloat16
AF = mybir.ActivationFunctionType
ALU = mybir.AluOpType

D = 1024
KD = 8
H = 16
QL = 256
KVL = 128
FH = 2816
NJ = 22
EPS = 1e-6
MLA_SCALE = 96 ** -0.5
NA_SCALE = 64 ** -0.5
GRID_W = 64
WIN_R = 8
WIN_C = 16
MASKV = -1000.0
NCORES = 8


_UID = [0]


def _uid():
    _UID[0] += 1
    return _UID[0]


class Res:
    __slots__ = ("last_w", "rd_eng", "rd_dma")

    def __init__(self):
        self.last_w = None
        self.rd_eng = {}
        self.rd_dma = []


class _Op:
    __slots__ = ("eng", "fn", "deps", "is_dma", "sem", "cnt")


ENGS = ["pe", "act", "dve", "pool", "sp"]
BLOCKNAME = {"pe": "tensor", "act": "scalar", "dve": "vector", "pool": "gpsimd", "sp": "sync"}
NDMASEM = {"sp": 16, "pool": 2, "act": 4}


class Prog:
    def __init__(self, nc):
        self.nc = nc
        self.ops = []
        self.per_eng = {e: [] for e in ENGS}
        self.dma_ids = {q: [] for q in NDMASEM}
        self.all_dma = []

    def op(self, eng, fn, r=(), w=(), dma=False, extra=()):
        o = _Op()
        o.eng, o.fn, o.is_dma, o.sem, o.cnt = eng, fn, dma, None, 0
        idx = len(self.ops)
        deps = set(extra)
        for x in r:
            if x.last_w is not None:
                deps.add(x.last_w)
        for x in w:
            if x.last_w is not None:
                deps.add(x.last_w)
            deps.update(x.rd_eng.values())
            deps.update(x.rd_dma)
        for x in r:
            if dma:
                x.rd_dma.append(idx)
            else:
                x.rd_eng[eng] = idx
        for x in w:
            x.last_w = idx
            x.rd_eng = {}
            x.rd_dma = []
        if dma:
            hist = self.dma_ids[eng]
            k = NDMASEM[eng]
            o.sem = (eng, len(hist) % k)
            o.cnt = 16 * (len(hist) // k + 1)
            if len(hist) >= k:
                deps.add(hist[len(hist) - k])
            hist.append(idx)
            self.all_dma.append(idx)
        deps.discard(idx)
        o.deps = deps
        self.ops.append(o)
        self.per_eng[eng].append(idx)
        return idx

    def dma(self, eng, out, in_, r=(), w=()):
        return self.op(eng, lambda e: e.dma_start(out=out, in_=in_), r, w, dma=True)

    def mm(self, out, lhsT, rhs, start=True, stop=True, r=(), w=(), skip=False):
        return self.op("pe", lambda e: e.matmul(out, lhsT, rhs, start=start, stop=stop, skip_group_check=skip), r, w)

    def tr(self, out, in_, ident, r=(), w=()):
        return self.op("pe", lambda e: e.transpose(out, in_, ident), r, w)

    def act(self, out, in_, func, r=(), w=(), **kw):
        return self.op("act", lambda e: e.activation(out, in_, func, **kw), r, w)

    def tt(self, eng, out, in0, in1, op, r=(), w=()):
        return self.op(eng, lambda e: e.tensor_tensor(out, in0, in1, op), r, w)

    def stt(self, out, in0, scalar, in1, op0, op1, r=(), w=()):
        return self.op("dve", lambda e: e.scalar_tensor_tensor(out, in0, scalar, in1, op0, op1), r, w)

    def ts(self, eng, out, in0, s1, s2, op0, op1=None, r=(), w=()):
        if op1 is None:
            return self.op(eng, lambda e: e.tensor_scalar(out, in0, s1, None, op0), r, w)
        return self.op(eng, lambda e: e.tensor_scalar(out, in0, s1, s2, op0, op1), r, w)

    def cp(self, eng, out, in_, r=(), w=()):
        if eng == "act":
            return self.op(eng, lambda e: e.copy(out, in_), r, w)
        return self.op(eng, lambda e: e.tensor_copy(out, in_), r, w)

    def recip(self, out, in_, r=(), w=()):
        return self.op("dve", lambda e: e.reciprocal(out, in_), r, w)

    def memset(self, eng, ap, val, w=()):
        return self.op(eng, lambda e: e.memset(ap, val), (), w)

    def emit(self):
        nc = self.nc
        ops = self.ops
        self.op("sp", None, extra=list(self.all_dma))
        flagged = set()
        for o in ops:
            flagged.update(o.deps)
        cnt = {e: 0 for e in ENGS}
        for i, o in enumerate(ops):
            if not o.is_dma and i in flagged:
                cnt[o.eng] += 1
                o.cnt = cnt[o.eng]
                o.sem = (o.eng, -1)
        sems = {}
        for e in ENGS:
            sems[(e, -1)] = nc.alloc_semaphore(name="se%d" % _uid())
        for q, k in NDMASEM.items():
            for s in range(min(k, len(self.dma_ids[q]))):
                sems[(q, s)] = nc.alloc_semaphore(name="sd%d" % _uid())
        with nc.Block() as block:
            for e in ENGS:
                ids = self.per_eng[e]
                if not ids:
                    continue

                def body(engine, e=e, ids=ids):
                    wm = {}
                    for i in ids:
                        o = ops[i]
                        need = {}
                        for d in o.deps:
                            od = ops[d]
                            if e == "pe" and od.eng == "pe" and not od.is_dma and not o.is_dma:
                                continue
                            if need.get(od.sem, 0) < od.cnt:
                                need[od.sem] = od.cnt
                        for s, c in need.items():
                            if wm.get(s, 0) < c:
                                engine.wait_ge(sems[s], c)
                                wm[s] = c
                        if o.fn is None:
                            continue
                        inst = o.fn(engine)
                        if o.is_dma:
                            inst.then_inc(sems[o.sem], 16)
                        elif i in flagged:
                            inst.then_inc(sems[o.sem], 1)

                getattr(block, BLOCKNAME[e])(body)
        nc.clear_and_free_semaphores(list(sems.values()))
        nc.all_engine_barrier()


class Ctx:
    def __init__(self, nc, es):
        self.nc, self.es, self.n = nc, es, 0

    def sb(self, shape, dt):
        self.n += 1
        return self.es.enter_context(self.nc.sbuf_tensor("sb%d" % _uid(), list(shape), dt))

    def ps(self, bf=False):
        self.n += 1
        if bf:
            return self.es.enter_context(self.nc.psum_tensor("ps%d" % _uid(), [128, 1024], BF16))
        return self.es.enter_context(self.nc.psum_tensor("ps%d" % _uid(), [128, 512], F32))

    def ps2(self):
        self.n += 1
        return self.es.enter_context(self.nc.psum_tensor("ps%d" % _uid(), [128, 1024], F32))


def rstd_ops(P, ss, v, rstd, nh, inv_n, Rss, Rv, Rrs, Rnh):
    P.ts("pool", v, ss, inv_n, EPS, ALU.mult, ALU.add, r=[Rss], w=[Rv])
    P.tt("pool", rstd, v, nh, ALU.pow, r=[Rv, Rnh], w=[Rrs])


def load_gain(P, C, row_ap, n=D):
    t = C.sb([128, n], F32)
    R = Res()
    P.dma("sp", t[:], row_ap.broadcast_to([128, n]), w=[R])
    return t, R


def phase0(nc, Dm, casts):
    with contextlib.ExitStack() as es:
        C = Ctx(nc, es)
        P = Prog(nc)
        for dst, src in casts:
            rows = dst.shape[0]
            for r0 in range(0, rows, 2048):
                r1 = min(rows, r0 + 2048)
                P.dma("pool", dst[r0:r1, :], src[r0:r1, :])
        cb = [C.sb([128, 2048], F32) for _ in range(2)]
        eb = [C.sb([128, 2048], BF16) for _ in range(2)]
        Rcb = [Res(), Res()]
        Reb = [Res(), Res()]
        for i in range(4):
            b = i % 2
            P.dma("sp", cb[b][:], Dm["cbm"][:, i * 2048:(i + 1) * 2048], w=[Rcb[b]])
            P.act(eb[b][:], cb[b][:], AF.Exp, r=[Rcb[b]], w=[Reb[b]])
            P.dma("sp", Dm["eb_d"][:, i * 2048:(i + 1) * 2048], eb[b][:], r=[Reb[b]])
        cT = C.sb([128, KD, 2], F32)
        cact = C.sb([128, KD, 2], F32)
        ones = C.sb([1, 2], F32)
        RcT, Rcact, Rones = Res(), Res(), Res()
        P.dma("sp", cT[:], Dm["cT"], w=[RcT])
        P.memset("pool", ones[:], 1.0, w=[Rones])
        P.act(cact[:], cT[:], AF.Silu, r=[RcT], w=[Rcact])
        wb = [C.sb([128, KD, 512], F32) for _ in range(2)]
        bb = [C.sb([1, 512], F32) for _ in range(2)]
        pm = [C.ps() for _ in range(2)]
        mo = [C.sb([2, 512], F32) for _ in range(2)]
        Rwb, Rbb, Rpm, Rmo = [Res(), Res()], [Res(), Res()], [Res(), Res()], [Res(), Res()]
        for l in range(2):
            wv = Dm["ada_w"][l].rearrange("(k p) n -> p k n", p=128)
            for n in range(12):
                b = (l * 12 + n) % 2
                P.dma("sp", wb[b][:], wv[:, :, n * 512:(n + 1) * 512], w=[Rwb[b]])
                P.dma("sp", bb[b][:], Dm["ada_b"][l:l + 1, n * 512:(n + 1) * 512], w=[Rbb[b]])
                for k in range(KD):
                    P.mm(pm[b][0:2, :], cact[:, k, :], wb[b][:, k, :], start=(k == 0), stop=False,
                         r=[Rcact, Rwb[b]], w=[Rpm[b]])
                P.mm(pm[b][0:2, :], ones[0:1, :], bb[b][0:1, :], start=False, stop=True,
                     r=[Rones, Rbb[b]], w=[Rpm[b]])
                P.cp("dve", mo[b][:], pm[b][0:2, :], r=[Rpm[b]], w=[Rmo[b]])
                P.dma("sp", Dm["mod_d"][l, :, n * 512:(n + 1) * 512], mo[b][:], r=[Rmo[b]])
        P.emit()


def mla_phaseA(nc, Dm, S, si, x_in, pers):
    nt = S // 128
    cqT, ckvT, KT = pers
    with contextlib.ExitStack() as es:
        C = Ctx(nc, es)
        P = Prog(nc)
        ident = C.sb([128, 128], BF16)
        Rid = Res()
        P.dma("sp", ident[:], Dm["ident"], w=[Rid])
        wd = C.sb([128, KD, 448], BF16)
        Rwd = Res()
        P.dma("sp", wd[:], Dm["wdkv_bf"].rearrange("(k p) n -> p k n", p=128), w=[Rwd])
        g, Rg = load_gain(P, C, Dm["norm_pre_mix"][0:1, :])
        sc, Rsc = load_gain(P, C, Dm["mod_d"][0, si:si + 1, 1 * D:2 * D])
        sh, Rsh = load_gain(P, C, Dm["mod_d"][0, si:si + 1, 0:D])
        gq, Rgq = load_gain(P, C, Dm["mla_q_norm"][0:1, :], QL)
        gkv, Rgkv = load_gain(P, C, Dm["mla_kv_norm"][0:1, :], KVL)
        G1 = C.sb([128, D], F32)
        RG1 = Res()
        P.stt(G1[:], sc[:], 1.0, g[:], ALU.add, ALU.mult, r=[Rsc, Rg], w=[RG1])
        ck = C.sb([128, nt, 32], F32)
        sk = C.sb([128, nt, 32], F32)
        Rck, Rsk = Res(), Res()
        P.dma("sp", ck[:], Dm["ck%d" % si], w=[Rck])
        P.dma("sp", sk[:], Dm["sk%d" % si], w=[Rsk])
        nh = C.sb([128, 2], F32)
        Rnh = Res()
        P.memset("pool", nh[:], -0.5, w=[Rnh])
        junk = C.sb([128, D], BF16)
        Rjunk = Res()
        xt = [C.sb([128, D], F32) for _ in range(2)]
        tmp = [C.sb([128, D], F32) for _ in range(2)]
        hb = [C.sb([128, D], BF16) for _ in range(2)]
        hT = [C.sb([128, KD, 128], BF16) for _ in range(2)]
        ltb = [C.sb([128, 416], BF16) for _ in range(2)]
        t1 = [C.sb([128, 32], F32) for _ in range(2)]
        t2 = [C.sb([128, 32], F32) for _ in range(2)]
        ss = [C.sb([128, 4], F32) for _ in range(2)]
        vv = [C.sb([128, 4], F32) for _ in range(2)]
        rs = [C.sb([128, 4], F32) for _ in range(2)]
        psT = [C.ps(bf=True) for _ in range(2)]
        psL = [C.ps() for _ in range(2)]
        psT2 = [C.ps(bf=True) for _ in range(2)]
        mk = lambda: [Res(), Res()]
        Rx, Rtmp, Rhb, RhT, Rltb, Rt1, Rt2, Rss, Rvv, Rrs, RpsT, RpsL, RpsT2 = [mk() for _ in range(13)]
        Rss2, Rvv2, Rrs2 = mk(), mk(), mk()
        def A1(t):
            b = t % 2
            tok = slice(t * 128, (t + 1) * 128)
            P.dma("sp", xt[b][:], x_in[tok, :], w=[Rx[b]])
            P.act(junk[:], xt[b][:], AF.Square, r=[Rx[b]], w=[Rjunk, Rss[b]], accum_out=ss[b][:, 0:1])
            rstd_ops(P, ss[b][:, 0:1], vv[b][:, 0:1], rs[b][:, 0:1], nh[:, 0:1], 1.0 / D, Rss[b], Rvv[b], Rrs[b], Rnh)
            P.stt(tmp[b][:], xt[b][:], rs[b][:, 0:1], G1[:], ALU.mult, ALU.mult, r=[Rx[b], Rrs[b], RG1], w=[Rtmp[b]])
            P.tt("pool", hb[b][:], tmp[b][:], sh[:], ALU.add, r=[Rtmp[b], Rsh], w=[Rhb[b]])

        def A2a(t):
            b = t % 2
            pT3 = psT[b][:].rearrange("p (k c) -> p k c", c=128)
            for k in range(KD):
                P.tr(pT3[:, k, :], hb[b][:, k * 128:(k + 1) * 128], ident[:], r=[Rhb[b], Rid], w=[RpsT[b]])
            P.cp("act", hT[b][:], pT3, r=[RpsT[b]], w=[RhT[b]])

        def A2b(t):
            b = t % 2
            for k in range(KD):
                P.mm(psL[b][:, 0:448], hT[b][:, k, :], wd[:, k, :], start=(k == 0), stop=(k == KD - 1),
                     r=[RhT[b], Rwd], w=[RpsL[b]])

        def A3(t):
            b = t % 2
            P.act(junk[:, 0:QL], psL[b][:, 0:QL], AF.Square, r=[RpsL[b]], w=[Rjunk, Rss2[b]], accum_out=ss[b][:, 1:2])
            P.act(junk[:, 0:KVL], psL[b][:, QL:QL + KVL], AF.Square, r=[RpsL[b]], w=[Rjunk, Rss2[b]],
                  accum_out=ss[b][:, 2:3])
            P.ts("pool", vv[b][:, 1:2], ss[b][:, 1:2], 1.0 / QL, EPS, ALU.mult, ALU.add, r=[Rss2[b]], w=[Rvv2[b]])
            P.ts("pool", vv[b][:, 2:3], ss[b][:, 2:3], 1.0 / KVL, EPS, ALU.mult, ALU.add, r=[Rss2[b]], w=[Rvv2[b]])
            P.tt("pool", rs[b][:, 1:3], vv[b][:, 1:3], nh[:, 0:2], ALU.pow, r=[Rvv2[b], Rnh], w=[Rrs2[b]])
            P.stt(ltb[b][:, 0:QL], psL[b][:, 0:QL], rs[b][:, 1:2], gq[:], ALU.mult, ALU.mult,
                  r=[RpsL[b], Rrs2[b], Rgq], w=[Rltb[b]])
            P.stt(ltb[b][:, QL:QL + KVL], psL[b][:, QL:QL + KVL], rs[b][:, 2:3], gkv[:], ALU.mult, ALU.mult,
                  r=[RpsL[b], Rrs2[b], Rgkv], w=[Rltb[b]])
            P.tt("dve", t1[b][:], psL[b][:, 384:416], ck[:, t, :], ALU.mult, r=[RpsL[b], Rck], w=[Rt1[b]])
            P.tt("dve", t2[b][:], psL[b][:, 416:448], sk[:, t, :], ALU.mult, r=[RpsL[b], Rsk], w=[Rt2[b]])
            P.tt("pool", ltb[b][:, 384:416], t1[b][:], t2[b][:], ALU.add, r=[Rt1[b], Rt2[b]], w=[Rltb[b]])

        def A4a(t):
            b = t % 2
            q3 = psT2[b][:].rearrange("p (k c) -> p k c", c=128)
            for k in range(3):
                P.tr(q3[:, k, :], ltb[b][:, k * 128:(k + 1) * 128], ident[:], r=[Rltb[b], Rid], w=[RpsT2[b]])
            P.tr(q3[:, 3, :], ltb[b][:, 288:416], ident[:], r=[Rltb[b], Rid], w=[RpsT2[b]])

        def A4b(t):
            b = t % 2
            tok = slice(t * 128, (t + 1) * 128)
            q3 = psT2[b][:].rearrange("p (k c) -> p k c", c=128)
            P.cp("dve", cqT[:, :, tok], q3[:, 0:2, :], r=[RpsT2[b]], w=[Res()])
            P.cp("dve", ckvT[:, tok], q3[:, 2, :], r=[RpsT2[b]], w=[Res()])
            P.cp("dve", KT[0][64:96, tok], q3[96:128, 3, :], r=[RpsT2[b]], w=[Res()])
            P.cp("dve", KT[1][64:96, tok], q3[96:128, 3, :], r=[RpsT2[b]], w=[Res()])

        A1(0)
        for t in range(nt + 1):
            if t < nt:
                A2a(t)
            if t >= 1:
                A4a(t - 1)
            if t < nt:
                A2b(t)
            if t + 1 < nt:
                A1(t + 1)
            if t >= 1:
                A4b(t - 1)
            if t < nt:
                A3(t)
        P.emit()


def mla_phaseB(nc, Dm, S, si, pers, bg=()):
    nt = S // 128
    QB = min(1024, S)
    NQ = QB // 512
    nqb = S // QB
    NSB = 3
    cqT, ckvT, KT = pers
    with contextlib.ExitStack() as es:
        C = Ctx(nc, es)
        P = Prog(nc)
        wuq = C.sb([128, 2, H * 192], BF16)
        wukv = C.sb([128, H * 128], BF16)
        Rwuq, Rwukv = Res(), Res()
        P.dma("sp", wuq[:], Dm["wuq_bf"].rearrange("(k p) n -> p k n", p=128), w=[Rwuq])
        P.dma("sp", wukv[:], Dm["wukv_bf"], w=[Rwukv])
        V = [C.sb([128, nt, 128], BF16) for _ in range(2)]
        RV = [Res(), Res()]
        for b in range(2):
            P.memset("pool", V[b][:], 1.0, w=[RV[b]])
        RK = [Res(), Res()]
        QT = [C.sb([96, QB], BF16) for _ in range(2)]
        ctab = [C.sb([96, QB], F32) for _ in range(2)]
        stab = [C.sb([96, QB], F32) for _ in range(2)]
        RQ, Rtab = [Res(), Res()], [Res(), Res()]
        t1 = C.sb([96, 512], F32)
        t2 = C.sb([96, 512], F32)
        Rt1, Rt2 = Res(), Res()
        PT = [C.sb([128, QB], BF16) for _ in range(3)]
        RPT = [Res(), Res(), Res()]
        psS = [C.ps2() for _ in range(NSB)]
        RpsS = [Res() for _ in range(NSB)]
        psO = C.ps2()
        RpsO = Res()
        Osb = C.sb([128, QB], F32)
        rsum = C.sb([64, QB], F32)
        OTn = [C.sb([64, QB], BF16) for _ in range(2)]
        ROsb, Rrsum, ROTn = Res(), Res(), [Res(), Res()]
        sc = [0]

        def new_slot():
            k = sc[0] % NSB
            sc[0] += 1
            return k

        def chunk_K(h, c0):
            def emit():
                k = new_slot()
                kb = h % 2
                for c in range(c0, min(c0 + 2, S // 512)):
                    bank = psS[k][:, (c - c0) * 512:(c - c0 + 1) * 512]
                    cs = slice(c * 512, (c + 1) * 512)
                    P.mm(bank, wukv[:, h * 128:(h + 1) * 128], ckvT[:, cs], r=[Rwukv], w=[RpsS[k]])
                    P.cp("dve", KT[kb][0:64, cs], bank[0:64, :], r=[RpsS[k]], w=[RK[kb]])
            return emit

        gsz = min(8, nt)

        def chunk_V(h, g0):
            def emit():
                k = new_slot()
                kb = h % 2
                for gi, g in enumerate(range(g0, min(g0 + 2 * gsz, nt), gsz)):
                    bank = psS[k][:, gi * 512:(gi + 1) * 512]
                    for j in range(gsz):
                        kt = g + j
                        P.mm(bank[:, j * 64:(j + 1) * 64], ckvT[:, kt * 128:(kt + 1) * 128],
                             wukv[:, h * 128 + 64:h * 128 + 128], r=[Rwukv], w=[RpsS[k]])
                    P.cp("dve", V[kb][:, g:g + gsz, 0:64], bank[:, 0:gsz * 64].rearrange("p (j d) -> p j d", d=64),
                         r=[RpsS[k]], w=[RV[kb]])
            return emit

        def chunk_Q(h, qb, qf, j):
            def emit():
                k = new_slot()
                pa, pb, Rk = psS[k][:, 0:512], psS[k][:, 512:1024], RpsS[k]
                if j == 0:
                    qs = slice(qb * QB, (qb + 1) * QB)
                    P.dma("sp", ctab[qf][64:96, :], Dm["cq%d" % si][:, qs], w=[Rtab[qf]])
                    P.dma("sp", stab[qf][64:96, :], Dm["sq%d" % si][:, qs], w=[Rtab[qf]])
                cs = slice(qb * QB + j * 512, qb * QB + (j + 1) * 512)
                js = slice(j * 512, (j + 1) * 512)
                for kk in range(2):
                    P.mm(pa[0:96, :], wuq[:, kk, h * 192:h * 192 + 96], cqT[:, kk, cs], start=(kk == 0), stop=(kk == 1),
                         r=[Rwuq], w=[Rk])
                for kk in range(2):
                    P.mm(pb[0:96, :], wuq[:, kk, h * 192 + 96:h * 192 + 192], cqT[:, kk, cs], start=(kk == 0),
                         stop=(kk == 1), r=[Rwuq], w=[Rk])
                P.ts("dve", QT[qf][0:64, js], pa[0:64, :], MLA_SCALE, None, ALU.mult, r=[Rk], w=[RQ[qf]])
                P.tt("dve", t1[64:96, :], pa[64:96, :], ctab[qf][64:96, js], ALU.mult, r=[Rk, Rtab[qf]], w=[Rt1])
                P.tt("dve", t2[64:96, :], pb[64:96, :], stab[qf][64:96, js], ALU.mult, r=[Rk, Rtab[qf]], w=[Rt2])
                P.tt("pool", QT[qf][64:96, js], t1[64:96, :], t2[64:96, :], ALU.add, r=[Rt1, Rt2], w=[RQ[qf]])
            return emit

        def kv_chunks(h):
            return [chunk_K(h, c0) for c0 in range(0, S // 512, 2)] + [chunk_V(h, g0) for g0 in range(0, nt, 2 * gsz)]

        def q_chunks(h, qb, qf):
            return [chunk_Q(h, qb, qf, j) for j in range(NQ)]

        def attend(h, qb, qf, ob, todo):
            kb = h % 2
            slot_of = {}
            points = {}
            for ch in todo:
                ch()

            def QK(kt):
                k = new_slot()
                slot_of[kt] = k
                for j in range(NQ):
                    js = slice(j * 512, (j + 1) * 512)
                    P.mm(psS[k][:, js], KT[kb][0:96, kt * 128:(kt + 1) * 128], QT[qf][0:96, js],
                         r=[RK[kb], RQ[qf]], w=[RpsS[k]])

            for kt in range(min(NSB, nt)):
                QK(kt)
            for kt in range(nt):
                k = slot_of[kt]
                P.act(PT[kt % 3][:], psS[k][:, 0:QB], AF.Exp, r=[RpsS[k]], w=[RPT[kt % 3]])
                for j in range(NQ):
                    js = slice(j * 512, (j + 1) * 512)
                    P.mm(psO[:, js], V[kb][:, kt, :], PT[kt % 3][:, js], start=(kt == 0), stop=(kt == nt - 1),
                         r=[RV[kb], RPT[kt % 3]], w=[RpsO])
                if kt + NSB < nt:
                    QK(kt + NSB)
                for ch in points.get(kt, []):
                    ch()
            P.cp("dve", Osb[:], psO[:, 0:QB], r=[RpsO], w=[ROsb])
            P.recip(rsum[0:64, :], Osb[64:128, :], r=[ROsb], w=[Rrsum])
            P.tt("dve", OTn[ob][:], Osb[0:64, :], rsum[0:64, :], ALU.mult, r=[ROsb, Rrsum], w=[ROTn[ob]])
            P.dma("sp", Dm["ot_d"][h * 64:(h + 1) * 64, qb * QB:(qb + 1) * QB], OTn[ob][:], r=[ROTn[ob]])

        items = [(h, qb) for h in range(H) for qb in range(nqb)]
        for ch in kv_chunks(0) + q_chunks(0, 0, 0):
            ch()
        kvq = []
        bg = list(bg)
        for i, (h, qb) in enumerate(items):
            if bg:
                d_, s_ = bg.pop(0)
                P.dma("pool", d_, s_)
            if qb == 0 and h + 1 < H:
                kvq = kv_chunks(h + 1)
            nkv = -(-len(kvq) // (nqb - qb))
            todo, kvq = kvq[:nkv], kvq[nkv:]
            if i + 1 < len(items):
                todo = todo + q_chunks(items[i + 1][0], items[i + 1][1], (i + 1) % 2)
            attend(h, qb, i % 2, i % 2, todo)
        for d_, s_ in bg:
            P.dma("pool", d_, s_)
        P.emit()


def mla_phaseC(nc, Dm, S, si, x_in, x_out):
    nt = S // 128
    TB = min(1024, S)
    with contextlib.ExitStack() as es:
        C = Ctx(nc, es)
        P = Prog(nc)
        wo = C.sb([128, KD, D], BF16)
        Rwo = Res()
        P.dma("sp", wo[:], Dm["wo_bf"].rearrange("(k p) n -> p k n", p=128), w=[Rwo])
        g, Rg = load_gain(P, C, Dm["norm_post_mix"][0:1, :])
        gt, Rgt = load_gain(P, C, Dm["mod_d"][0, si:si + 1, 2 * D:3 * D])
        G2 = C.sb([128, D], F32)
        RG2 = Res()
        P.tt("dve", G2[:], g[:], gt[:], ALU.mult, r=[Rg, Rgt], w=[RG2])
        nh = C.sb([128, 1], F32)
        Rnh = Res()
        P.memset("pool", nh[:], -0.5, w=[Rnh])
        junk = C.sb([128, D], BF16)
        Rjunk = Res()
        otv = Dm["ot_d"].rearrange("(k p) s -> p k s", p=128)
        OTt = [C.sb([128, KD, TB], BF16) for _ in range(2)]
        xt = [C.sb([128, D], F32) for _ in range(2)]
        tmp = [C.sb([128, D], F32) for _ in range(2)]
        ss = [C.sb([128, 1], F32) for _ in range(2)]
        vv = [C.sb([128, 1], F32) for _ in range(2)]
        rs = [C.sb([128, 1], F32) for _ in range(2)]
        psY = [C.ps2() for _ in range(2)]
        mk = lambda: [Res(), Res()]
        ROT, Rx, Rtmp, Rss, Rvv, Rrs, RpsY = [mk() for _ in range(7)]
        for t in range(nt):
            b = t % 2
            blk = (t * 128) // TB
            bb = blk % 2
            if (t * 128) % TB == 0:
                if blk == 0:
                    P.dma("sp", OTt[0][:], otv[:, :, 0:TB], w=[ROT[0]])
                if (blk + 1) * TB < S:
                    P.dma("sp", OTt[(blk + 1) % 2][:], otv[:, :, (blk + 1) * TB:(blk + 2) * TB], w=[ROT[(blk + 1) % 2]])
            o0 = (t * 128) % TB
            tok = slice(t * 128, (t + 1) * 128)
            P.dma("sp", xt[b][:], x_in[tok, :], w=[Rx[b]])
            for n in range(2):
                for k in range(KD):
                    P.mm(psY[b][:, n * 512:(n + 1) * 512], OTt[bb][:, k, o0:o0 + 128], wo[:, k, n * 512:(n + 1) * 512],
                         start=(k == 0), stop=(k == KD - 1), r=[ROT[bb], Rwo], w=[RpsY[b]])
            P.act(junk[:], psY[b][:], AF.Square, r=[RpsY[b]], w=[Rjunk, Rss[b]], accum_out=ss[b][:, 0:1])
            rstd_ops(P, ss[b][:], vv[b][:], rs[b][:], nh[:], 1.0 / D, Rss[b], Rvv[b], Rrs[b], Rnh)
            P.stt(tmp[b][:], psY[b][:], rs[b][:, 0:1], G2[:], ALU.mult, ALU.mult, r=[RpsY[b], Rrs[b], RG2], w=[Rtmp[b]])
            P.tt("pool", xt[b][:], tmp[b][:], xt[b][:], ALU.add, r=[Rtmp[b], Rx[b]], w=[Rx[b]])
            P.dma("sp", x_out[tok, :], xt[b][:], r=[Rx[b]])
        P.emit()


def ffn_phase(nc, Dm, S, si, l, x_in, x_out):
    TB = 256
    nb = S // TB
    with contextlib.ExitStack() as es:
        C = Ctx(nc, es)
        P = Prog(nc)
        ident = C.sb([128, 128], BF16)
        Rid = Res()
        P.dma("sp", ident[:], Dm["ident"], w=[Rid])
        wgu = C.sb([128, KD, 2 * FH], BF16)
        wdn = C.sb([128, NJ, D], BF16)
        Rwgu, Rwdn = Res(), Res()
        gv = Dm["wgu_bf"][l].rearrange("(k p) n -> p k n", p=128)
        for k in range(KD):
            P.dma("sp", wgu[:, k, :], gv[:, k, :], w=[Rwgu])
        dv = Dm["wdn_bf"][l].rearrange("(j p) n -> p j n", p=128)
        for j0 in range(0, NJ, 11):
            P.dma("sp", wdn[:, j0:j0 + 11, :], dv[:, j0:j0 + 11, :], w=[Rwdn])
        G3 = C.sb([128, D], F32)
        G4 = C.sb([128, D], F32)
        RG3, RG4 = Res(), Res()
        sh, Rsh = load_gain(P, C, Dm["mod_d"][l, si:si + 1, 3 * D:4 * D])
        nh = C.sb([128, 1], F32)
        Rnh = Res()
        P.memset("pool", nh[:], -0.5, w=[Rnh])
        junk = C.sb([128, D], BF16)
        Rjunk = Res()
        xres = [C.sb([128, 2, D], F32) for _ in range(2)]
        tmp = [C.sb([128, D], F32) for _ in range(2)]
        hb = [C.sb([128, D], BF16) for _ in range(2)]
        hT = [C.sb([128, KD, TB], BF16) for _ in range(2)]
        actT = C.sb([128, NJ, TB], BF16)
        sg = [C.sb([128, TB], F32) for _ in range(2)]
        ss = [C.sb([128, 2], F32) for _ in range(4)]
        vv = [C.sb([128, 2], F32) for _ in range(4)]
        rs = [C.sb([128, 2], F32) for _ in range(4)]
        psT = [C.ps(bf=True) for _ in range(2)]
        psGU = [C.ps() for _ in range(2)]
        psY = [C.ps2() for _ in range(2)]
        mk = lambda n=2: [Res() for _ in range(n)]
        Rx = [mk(), mk()]
        Rtmp, Rhb, RhT, Rsg, RpsT, RpsGU, RpsY = [mk() for _ in range(7)]
        Rss, Rvv, Rrs, Rss2, Rvv2, Rrs2 = [mk(4) for _ in range(6)]
        RactT = Res()
        tmp2 = [C.sb([128, D], F32) for _ in range(2)]
        Rtmp2 = mk()
        P.dma("sp", G3[:], Dm["norm_pre_ffn"][l:l + 1, :].broadcast_to([128, D]), w=[RG3])
        P.dma("sp", tmp[0][:], Dm["mod_d"][l, si:si + 1, 4 * D:5 * D].broadcast_to([128, D]), w=[Rtmp[0]])
        P.stt(G3[:], tmp[0][:], 1.0, G3[:], ALU.add, ALU.mult, r=[Rtmp[0], RG3], w=[RG3])
        P.dma("sp", G4[:], Dm["norm_post_ffn"][l:l + 1, :].broadcast_to([128, D]), w=[RG4])
        P.dma("sp", tmp[1][:], Dm["mod_d"][l, si:si + 1, 5 * D:6 * D].broadcast_to([128, D]), w=[Rtmp[1]])
        P.tt("dve", G4[:], G4[:], tmp[1][:], ALU.mult, r=[RG4, Rtmp[1]], w=[RG4])

        def pre_a(b):
            bb = b % 2
            for i in range(2):
                t = 2 * b + i
                q = t % 4
                tok = slice(t * 128, (t + 1) * 128)
                P.dma("sp", xres[bb][:, i, :], x_in[tok, :], w=[Rx[bb][i]])
                P.act(junk[:], xres[bb][:, i, :], AF.Square, r=[Rx[bb][i]], w=[Rjunk, Rss[q]], accum_out=ss[q][:, 0:1])
                rstd_ops(P, ss[q][:, 0:1], vv[q][:, 0:1], rs[q][:, 0:1], nh[:], 1.0 / D, Rss[q], Rvv[q], Rrs[q], Rnh)
                P.stt(tmp[i][:], xres[bb][:, i, :], rs[q][:, 0:1], G3[:], ALU.mult, ALU.mult,
                      r=[Rx[bb][i], Rrs[q], RG3], w=[Rtmp[i]])
                P.tt("pool", hb[i][:], tmp[i][:], sh[:], ALU.add, r=[Rtmp[i], Rsh], w=[Rhb[i]])

        def pre_b(b):
            bb = b % 2
            for i in range(2):
                pT3 = psT[i][:].rearrange("p (k c) -> p k c", c=128)
                for k in range(KD):
                    P.tr(pT3[:, k, :], hb[i][:, k * 128:(k + 1) * 128], ident[:], r=[Rhb[i], Rid], w=[RpsT[i]])
                P.cp("dve", hT[bb][:, :, i * 128:(i + 1) * 128], pT3, r=[RpsT[i]], w=[RhT[bb]])

        def gu(b):
            bb = b % 2
            for j in range(NJ):
                p = j % 2
                for k in range(KD):
                    P.mm(psGU[p][:, 0:TB], wgu[:, k, j * 128:(j + 1) * 128], hT[bb][:, k, :], start=(k == 0),
                         stop=(k == KD - 1), r=[Rwgu, RhT[bb]], w=[RpsGU[p]])
                for k in range(KD):
                    P.mm(psGU[p][:, TB:2 * TB], wgu[:, k, FH + j * 128:FH + (j + 1) * 128], hT[bb][:, k, :],
                         start=(k == 0), stop=(k == KD - 1), r=[Rwgu, RhT[bb]], w=[RpsGU[p]])
                P.act(sg[p][:], psGU[p][:, 0:TB], AF.Silu, r=[RpsGU[p]], w=[Rsg[p]])
                P.tt("dve", actT[:, j, :], sg[p][:], psGU[p][:, TB:2 * TB], ALU.mult, r=[Rsg[p], RpsGU[p]], w=[RactT])

        def down_post(b, i):
            bb = b % 2
            t = 2 * b + i
            q = t % 4
            tok = slice(t * 128, (t + 1) * 128)
            for n in range(2):
                for j in range(NJ):
                    P.mm(psY[i][:, n * 512:(n + 1) * 512], actT[:, j, i * 128:(i + 1) * 128],
                         wdn[:, j, n * 512:(n + 1) * 512], start=(j == 0), stop=(j == NJ - 1),
                         r=[RactT, Rwdn], w=[RpsY[i]])
            P.act(junk[:], psY[i][:], AF.Square, r=[RpsY[i]], w=[Rjunk, Rss2[q]], accum_out=ss[q][:, 1:2])
            rstd_ops(P, ss[q][:, 1:2], vv[q][:, 1:2], rs[q][:, 1:2], nh[:], 1.0 / D, Rss2[q], Rvv2[q], Rrs2[q], Rnh)
            P.stt(tmp2[i][:], psY[i][:], rs[q][:, 1:2], G4[:], ALU.mult, ALU.mult, r=[RpsY[i], Rrs2[q], RG4], w=[Rtmp2[i]])
            P.tt("pool", xres[bb][:, i, :], tmp2[i][:], xres[bb][:, i, :], ALU.add, r=[Rtmp2[i], Rx[bb][i]], w=[Rx[bb][i]])
            P.dma("sp", x_out[tok, :], xres[bb][:, i, :], r=[Rx[bb][i]])

        pre_a(0)
        pre_b(0)
        for b in range(nb):
            gu(b)
            if b + 1 < nb:
                pre_a(b + 1)
            down_post(b, 0)
            if b + 1 < nb:
                pre_b(b + 1)
            down_post(b, 1)
        P.emit()


def na_plan(S):
    rows = S // GRID_W
    nt = rows // 2
    rstart = lambda r: min(max(r - WIN_R // 2, 0), rows - WIN_R)
    plan = []
    for i in range(nt):
        lo = min(rstart(2 * i), rstart(2 * i + 1))
        hi = max(rstart(2 * i), rstart(2 * i + 1)) + WIN_R - 1
        kts = []
        for kt in range(lo // 2, hi // 2 + 1):
            key = []
            for jr in range(2):
                for qr in range(2):
                    kr, q = 2 * kt + jr, 2 * i + qr
                    ok = rstart(q) <= kr < rstart(q) + WIN_R
                    key.append(kr - q + WIN_R - 1 if ok else 15)
            kts.append((kt, tuple(key)))
        plan.append(kts)
    return plan


def na_phase(nc, Dm, S, si, x_in, x_out):
    nt = S // 128
    LAG = 3
    NSL = 8
    plan = na_plan(S)
    interior = [tuple(2 * off + jr - qr + 7 if -4 <= 2 * off + jr - qr <= 3 else 15 for jr in range(2) for qr in range(2))
                for off in range(-2, 3)]
    with contextlib.ExitStack() as es:
        C = Ctx(nc, es)
        P = Prog(nc)
        ident = C.sb([128, 128], BF16)
        Rid = Res()
        P.dma("sp", ident[:], Dm["ident"], w=[Rid])
        wq = C.sb([128, KD, 3 * D], BF16)
        wo = C.sb([128, KD, D], BF16)
        Rwq, Rwo = Res(), Res()
        qv = Dm["nqkv_bf"].rearrange("(k p) n -> p k n", p=128)
        for k in range(KD):
            P.dma("sp", wq[:, k, :], qv[:, k, :], w=[Rwq])
        P.dma("sp", wo[:], Dm["nwo_bf"].rearrange("(k p) n -> p k n", p=128), w=[Rwo])
        G5 = C.sb([128, D], F32)
        G6 = C.sb([128, D], F32)
        RG5, RG6 = Res(), Res()
        sh, Rsh = load_gain(P, C, Dm["mod_d"][1, si:si + 1, 0:D])
        nh = C.sb([128, 1], F32)
        Rnh = Res()
        P.memset("pool", nh[:], -0.5, w=[Rnh])
        junk = C.sb([128, D], BF16)
        Rjunk = Res()
        NPAT = 9
        pat = [C.sb([128, 2, H, 64], BF16) for _ in range(NPAT)]
        Rpat = [Res() for _ in range(NPAT)]
        ebv = Dm["eb_d"].rearrange("(e a) (b h q) -> e (a b) h q", e=16, h=H, q=64)

        def load_pat(slot, key):
            for jr in range(2):
                for qr in range(2):
                    P.dma("sp", pat[slot][jr * 64:(jr + 1) * 64, qr, :, :], ebv[key[jr * 2 + qr]], w=[Rpat[slot]])

        slot_of = {}
        for s_, key in enumerate(interior):
            load_pat(s_, key)
            slot_of[key] = s_
        rot = [0]

        def get_pat(key):
            if key in slot_of:
                return slot_of[key]
            s_ = 5 + rot[0] % (NPAT - 5)
            rot[0] += 1
            load_pat(s_, key)
            return s_

        KTr = C.sb([128, KD, NSL, 128], BF16)
        QT2 = C.sb([128, KD, 4, 2, 128], BF16)
        Vr = C.sb([128, NSL, H, 65], BF16)
        RKs = [Res() for _ in range(NSL)]
        RVs = [Res() for _ in range(NSL)]
        RQs = [Res() for _ in range(4)]
        RVall = Res()
        P.memset("pool", Vr[:], 1.0, w=[RVall] + RVs)
        P.memset("pool", QT2[:], 0.0, w=RQs)
        xt = [C.sb([128, D], F32) for _ in range(2)]
        tmp = [C.sb([128, D], F32) for _ in range(2)]
        hb = [C.sb([128, D], BF16) for _ in range(2)]
        hT = [C.sb([128, KD, 128], BF16) for _ in range(2)]
        exS = [C.sb([128, 512], BF16) for _ in range(2)]
        PTt = [C.sb([128, 512], BF16) for _ in range(3)]
        On = [C.sb([128, D], BF16) for _ in range(2)]
        tmp2 = C.sb([128, D], F32)
        Rtmp2 = Res()
        OT = C.sb([128, KD, 128], BF16)
        xr = C.sb([128, D], F32)
        rsm = C.sb([128, H], F32)
        ss = [C.sb([128, 2], F32) for _ in range(2)]
        vv = [C.sb([128, 2], F32) for _ in range(2)]
        rs = [C.sb([128, 2], F32) for _ in range(2)]
        psT = C.ps(bf=True)
        psQK = C.ps2()
        psV = C.ps2()
        psS = [C.ps(), C.ps()]
        psO3 = C.ps()
        RpsT, RpsQK, RpsV, RpsO3 = Res(), Res(), Res(), Res()
        RpsS = [Res(), Res()]
        mk = lambda: [Res(), Res()]
        Rx, Rtmp, RhT, RexS, Rss, Rvv, Rrs, Rss2, Rvv2, Rrs2 = [mk() for _ in range(10)]
        ROT, Rxr, Rrsm = Res(), Res(), Res()
        Rhb = [Res(), Res()]
        ROn = [Res(), Res()]
        RPT = [Res(), Res(), Res()]

        P.dma("sp", G5[:], Dm["norm_pre_mix"][1:2, :].broadcast_to([128, D]), w=[RG5])
        P.dma("sp", tmp[0][:], Dm["mod_d"][1, si:si + 1, 1 * D:2 * D].broadcast_to([128, D]), w=[Rtmp[0]])
        P.stt(G5[:], tmp[0][:], 1.0, G5[:], ALU.add, ALU.mult, r=[Rtmp[0], RG5], w=[RG5])
        P.dma("sp", G6[:], Dm["norm_post_mix"][1:2, :].broadcast_to([128, D]), w=[RG6])
        P.dma("sp", tmp[1][:], Dm["mod_d"][1, si:si + 1, 2 * D:3 * D].broadcast_to([128, D]), w=[Rtmp[1]])
        P.tt("dve", G6[:], G6[:], tmp[1][:], ALU.mult, r=[RG6, Rtmp[1]], w=[RG6])

        def obank(h):
            if h < 7:
                return psV[:, h * 65:(h + 1) * 65], RpsV
            if h < 14:
                return psV[:, 512 + (h - 7) * 65:512 + (h - 6) * 65], RpsV
            return psO3[:, (h - 14) * 65:(h - 13) * 65], RpsO3

        def pre(t):
            b = t % 2
            tok = slice(t * 128, (t + 1) * 128)
            P.dma("sp", xt[b][:], x_in[tok, :], w=[Rx[b]])
            P.act(junk[:], xt[b][:], AF.Square, r=[Rx[b]], w=[Rjunk, Rss[b]], accum_out=ss[b][:, 0:1])
            rstd_ops(P, ss[b][:, 0:1], vv[b][:, 0:1], rs[b][:, 0:1], nh[:], 1.0 / D, Rss[b], Rvv[b], Rrs[b], Rnh)
            P.stt(tmp[b][:], xt[b][:], rs[b][:, 0:1], G5[:], ALU.mult, ALU.mult, r=[Rx[b], Rrs[b], RG5], w=[Rtmp[b]])
            P.tt("pool", hb[b][:], tmp[b][:], sh[:], ALU.add, r=[Rtmp[b], Rsh], w=[Rhb[b]])

        def stage1(t):
            b = t % 2
            sl = t % NSL
            qsl = t % 4
            pT3 = psT[:].rearrange("p (k c) -> p k c", c=128)
            for k in range(KD):
                P.tr(pT3[:, k, :], hb[b][:, k * 128:(k + 1) * 128], ident[:], r=[Rhb[b], Rid], w=[RpsT])
            P.cp("dve", hT[b][:], pT3, r=[RpsT], w=[RhT[b]])
            q4 = psQK[:].rearrange("p (c q) -> p c q", q=128)
            for part in range(2):
                for c in range(KD):
                    for k in range(KD):
                        P.mm(q4[:, c, :], wq[:, k, part * D + c * 128:part * D + (c + 1) * 128], hT[b][:, k, :],
                             start=(k == 0), stop=(k == KD - 1), r=[Rwq, RhT[b]], w=[RpsQK])
                if part == 1:
                    P.cp("dve", KTr[:, :, sl, :], q4, r=[RpsQK], w=[RKs[sl]])
                else:
                    P.ts("dve", QT2[0:64, :, qsl, 0, :], q4[0:64, :, :], NA_SCALE, None, ALU.mult, r=[RpsQK], w=[RQs[qsl]])
                    P.ts("dve", QT2[64:128, :, qsl, 1, :], q4[64:128, :, :], NA_SCALE, None, ALU.mult, r=[RpsQK], w=[RQs[qsl]])
            for n in range(2):
                for k in range(KD):
                    P.mm(psV[:, n * 512:(n + 1) * 512], hT[b][:, k, :], wq[:, k, 2 * D + n * 512:2 * D + (n + 1) * 512],
                         start=(k == 0), stop=(k == KD - 1), r=[RhT[b], Rwq], w=[RpsV])
            P.cp("dve", Vr[:, sl, :, 0:64], psV[:].rearrange("p (h d) -> p h d", d=64), r=[RpsV], w=[RVs[sl]])

        def stage2a(i):
            qsl = i % 4
            kts = plan[i]
            slots = [get_pat(key) for _, key in kts]
            steps = [(g, n_kt) for g in range(4) for n_kt in range(len(kts))]
            ns = len(steps)

            def ST(s_):
                g, n_kt = steps[s_]
                sl = kts[n_kt][0] % NSL
                for pp in range(2):
                    c = 2 * g + pp
                    P.mm(psS[s_ % 2][:, pp * 256:(pp + 1) * 256], KTr[:, c, sl, :],
                         QT2[:, c, qsl, :, :].rearrange("p e q -> p (e q)"), r=[RKs[sl], RQs[qsl]], w=[RpsS[s_ % 2]])

            def EM(s_):
                g, n_kt = steps[s_]
                sb_, pb_ = s_ % 2, s_ % 3
                P.act(exS[sb_][:], psS[sb_][:], AF.Exp, r=[RpsS[sb_]], w=[RexS[sb_]])
                P.tt("pool" if s_ % 4 == 3 else "dve", PTt[pb_][:].rearrange("p (h r c) -> p h r c", h=4, r=2),
                     exS[sb_][:].rearrange("p (h r c) -> p h r c", h=4, r=2),
                     pat[slots[n_kt]][:, :, 4 * g:4 * g + 4, :].rearrange("p r h c -> p h r c"), ALU.mult,
                     r=[RexS[sb_], Rpat[slots[n_kt]]], w=[RPT[pb_]])

            def PV(s_):
                g, n_kt = steps[s_]
                sl = kts[n_kt][0] % NSL
                for hh in range(4):
                    h = 4 * g + hh
                    oap, Ro = obank(h)
                    P.mm(oap, PTt[s_ % 3][:, hh * 128:(hh + 1) * 128], Vr[:, sl, h, :],
                         start=(h in (0, 7, 14) and n_kt == 0), stop=(n_kt == len(kts) - 1),
                         r=[RPT[s_ % 3], RVs[sl]], w=[Ro], skip=True)

            ST(0)
            if ns > 1:
                ST(1)
            for s_ in range(ns):
                EM(s_)
                PV(s_)
                if s_ + 2 < ns:
                    ST(s_ + 2)
            ob = i % 2
            banks = ((psV[:, 0:455], RpsV, 0, 7), (psV[:, 512:967], RpsV, 7, 7), (psO3[:, 0:130], RpsO3, 14, 2))
            rs3 = rsm[:].rearrange("p (h o) -> p h o", o=1)
            On3 = On[ob][:].rearrange("p (h d) -> p h d", d=64)
            for ap, Rb, h0, nh_ in banks:
                a3 = ap.rearrange("p (h e) -> p h e", e=65)
                P.recip(rs3[:, h0:h0 + nh_, :], a3[:, :, 64:65], r=[Rb], w=[Rrsm])
                P.tt("dve", On3[:, h0:h0 + nh_, :], a3[:, :, 0:64], rs3[:, h0:h0 + nh_, :].broadcast_to([128, nh_, 64]),
                     ALU.mult, r=[Rb, Rrsm], w=[ROn[ob]])

        def stage2b(i):
            ob = i % 2
            tok = slice(i * 128, (i + 1) * 128)
            pT3 = psT[:].rearrange("p (k c) -> p k c", c=128)
            for k in range(KD):
                P.tr(pT3[:, k, :], On[ob][:, k * 128:(k + 1) * 128], ident[:], r=[ROn[ob], Rid], w=[RpsT])
            P.cp("dve", OT[:], pT3, r=[RpsT], w=[ROT])
            for n in range(2):
                for k in range(KD):
                    P.mm(psQK[:, n * 512:(n + 1) * 512], OT[:, k, :], wo[:, k, n * 512:(n + 1) * 512], start=(k == 0),
                         stop=(k == KD - 1), r=[ROT, Rwo], w=[RpsQK])
            b = i % 2
            P.dma("sp", xr[:], x_in[tok, :], w=[Rxr])
            P.act(junk[:], psQK[:], AF.Square, r=[RpsQK], w=[Rjunk, Rss2[b]], accum_out=ss[b][:, 1:2])
            rstd_ops(P, ss[b][:, 1:2], vv[b][:, 1:2], rs[b][:, 1:2], nh[:], 1.0 / D, Rss2[b], Rvv2[b], Rrs2[b], Rnh)
            P.stt(tmp2[:], psQK[:], rs[b][:, 1:2], G6[:], ALU.mult, ALU.mult, r=[RpsQK, Rrs2[b], RG6], w=[Rtmp2])
            P.tt("pool", xr[:], tmp2[:], xr[:], ALU.add, r=[Rtmp2, Rxr], w=[Rxr])
            P.dma("sp", x_out[tok, :], xr[:], r=[Rxr])

        pre(0)
        for t in range(nt + LAG + 1):
            if t < nt:
                stage1(t)
            if t + 1 < nt:
                pre(t + 1)
            if LAG <= t < nt + LAG:
                stage2a(t - LAG)
            if t >= LAG + 1:
                stage2b(t - LAG - 1)
        P.emit()


def build(SP, SS):
    nc = bass.Bass("TRN2", target_bir_lowering=False)
    Dm = {}

    def din(name, shape, dt=F32):
        Dm[name] = nc.dram_tensor(name, list(shape), dt, kind="ExternalInput").ap()

    def dscr(name, shape, dt):
        Dm[name] = nc.dram_tensor(name, list(shape), dt, kind="Internal").ap()

    seqs = [(SP, "xp", "yp"), (SS, "xs", "ys")]
    din("xp", [SP, D])
    din("xs", [SS, D])
    Dm["yp"] = nc.dram_tensor("yp", [SP, D], F32, kind="ExternalOutput").ap()
    Dm["ys"] = nc.dram_tensor("ys", [SS, D], F32, kind="ExternalOutput").ap()
    din("cT", [128, KD, 2])
    din("ada_w", [2, D, 6 * D])
    din("ada_b", [2, 6 * D])
    for n in ("norm_pre_mix", "norm_post_mix", "norm_pre_ffn", "norm_post_ffn"):
        din(n, [2, D])
    din("mla_q_norm", [1, QL])
    din("mla_kv_norm", [1, KVL])
    din("ident", [128, 128], BF16)
    din("cbm", [128, 8192])
    wspecs = [("wdkv", [D, 448]), ("wuq", [QL, H * 192]), ("wukv", [KVL, H * 128]), ("wo", [D, D]),
              ("nqkv", [D, 3 * D]), ("nwo", [D, D]), ("wgu", [2, D, 2 * FH]), ("wdn", [2, FH, D])]
    casts = []
    for n, shp in wspecs:
        din(n + "_f", shp)
        dscr(n + "_bf", shp, BF16)
        tot = int(np.prod(shp))
        fl = " ".join("abc"[:len(shp)])
        pat = "%s -> (%s)" % (fl, fl)
        src = Dm[n + "_f"].rearrange(pat).rearrange("(r c) -> r c", c=64 if tot % 512 else 512)
        dst = Dm[n + "_bf"].rearrange(pat).rearrange("(r c) -> r c", c=64 if tot % 512 else 512)
        casts.append((dst, src))
    for si, (S, _, _) in enumerate(seqs):
        nt = S // 128
        din("ck%d" % si, [128, nt, 32])
        din("sk%d" % si, [128, nt, 32])
        din("cq%d" % si, [32, S])
        din("sq%d" % si, [32, S])
    dscr("mod_d", [2, 2, 6 * D], F32)
    dscr("eb_d", [128, 8192], BF16)
    dscr("ot_d", [D, max(SP, SS)], BF16)
    dscr("xa", [max(SP, SS), D], F32)
    dscr("xb", [max(SP, SS), D], F32)

    phase0(nc, Dm, casts[:3])
    deferred = []
    for dst, src in casts[3:]:
        for r0 in range(0, dst.shape[0], 2048):
            r1 = min(dst.shape[0], r0 + 2048)
            deferred.append((dst[r0:r1, :], src[r0:r1, :]))
    for si, (S, xn, yn) in enumerate(seqs):
        x_in, y_out = Dm[xn], Dm[yn]
        xa, xb = Dm["xa"][0:S, :], Dm["xb"][0:S, :]
        saved_ot = Dm["ot_d"]
        Dm["ot_d"] = saved_ot[:, 0:S]
        with contextlib.ExitStack() as es:
            cqT = es.enter_context(nc.sbuf_tensor("cqT%d" % si, [128, 2, S], BF16))
            ckvT = es.enter_context(nc.sbuf_tensor("ckvT%d" % si, [128, S], BF16))
            KT = [es.enter_context(nc.sbuf_tensor("KT%d_%d" % (si, b), [96, S], BF16)) for b in range(2)]
            pers = (cqT, ckvT, KT)
            mla_phaseA(nc, Dm, S, si, x_in, pers)
            mla_phaseB(nc, Dm, S, si, pers, bg=deferred if si == 0 else ())
        mla_phaseC(nc, Dm, S, si, x_in, xa)
        Dm["ot_d"] = saved_ot
        ffn_phase(nc, Dm, S, si, 0, xa, xb)
        na_phase(nc, Dm, S, si, xb, xa)
        ffn_phase(nc, Dm, S, si, 1, xa, y_out)
    return nc


def _rope_tabs(S):
    inv = (1.0 / (np.float32(10000.0) ** (np.arange(0, 32, 2, dtype=np.float32) / np.float32(32)))).astype(np.float32)
    ang = (np.arange(S, dtype=np.float32)[:, None] * inv[None, :]).astype(np.float32)
    cos, sin = np.cos(ang).astype(np.float32), np.sin(ang).astype(np.float32)
    c2 = np.concatenate([cos, cos], 1)
    s2 = np.concatenate([-sin, sin], 1)
    nt = S // 128
    ck = np.ascontiguousarray(c2.reshape(nt, 128, 32).transpose(1, 0, 2))
    sk = np.ascontiguousarray(s2.reshape(nt, 128, 32).transpose(1, 0, 2))
    cq = np.ascontiguousarray((c2 * np.float32(MLA_SCALE)).T)
    sq = np.ascontiguousarray((s2 * np.float32(MLA_SCALE)).T)
    return ck, sk, cq, sq


_CACHE = {}


def kernel(x_prompt, x_sample, c_prompt, c_sample, ada_w, ada_b, norm_pre_mix, norm_post_mix, norm_pre_ffn,
           norm_post_ffn, mla_w_dkv, mla_q_norm, mla_kv_norm, mla_w_uq, mla_w_ukv, mla_w_o, na_w_qkv, na_rpb, na_w_o,
           ffn_w_gu, ffn_w_down):
    f = lambda a: np.ascontiguousarray(np.asarray(a, dtype=np.float32))
    x_prompt, x_sample = f(x_prompt), f(x_sample)
    SP, SS = x_prompt.shape[1], x_sample.shape[1]
    if (SP, SS) not in _CACHE:
        _CACHE[(SP, SS)] = build(SP, SS)
    nc = _CACHE[(SP, SS)]
    wd = f(mla_w_dkv)[0]
    wdkv = np.concatenate([wd, wd[:, 400:416], wd[:, 384:400]], 1)
    wq = f(mla_w_uq)[0].reshape(QL, H, 96)
    wuq = np.concatenate([wq, wq[:, :, 0:64], wq[:, :, 80:96], wq[:, :, 64:80]], 2).reshape(QL, H * 192)
    rpb = f(na_rpb)[0]
    cidx = np.arange(GRID_W)
    cstart = np.clip(cidx - WIN_C // 2, 0, GRID_W - WIN_C)
    valid = (cidx[None, :] >= cstart[:, None]) & (cidx[None, :] < cstart[:, None] + WIN_C)
    dc = np.clip(cidx[None, :] - cidx[:, None] + WIN_C - 1, 0, 2 * WIN_C - 2)
    cb = rpb[:, :, dc]
    cb = np.where(valid[None, None], cb, np.float32(MASKV))
    cbm = np.full((16, 64, H, 64), MASKV, np.float32)
    cbm[:15] = cb.transpose(1, 3, 0, 2)
    shared = {
        "ada_w": f(ada_w), "ada_b": f(ada_b), "norm_pre_mix": f(norm_pre_mix), "norm_post_mix": f(norm_post_mix),
        "norm_pre_ffn": f(norm_pre_ffn), "norm_post_ffn": f(norm_post_ffn), "mla_q_norm": f(mla_q_norm),
        "mla_kv_norm": f(mla_kv_norm), "ident": np.eye(128, dtype=np.float32).astype(ml_dtypes.bfloat16),
        "cbm": np.ascontiguousarray(cbm.reshape(128, 8192)), "wdkv_f": np.ascontiguousarray(wdkv),
        "wuq_f": np.ascontiguousarray(wuq), "wukv_f": f(mla_w_ukv)[0], "wo_f": f(mla_w_o)[0], "nqkv_f": f(na_w_qkv)[0],
        "nwo_f": f(na_w_o)[0], "wgu_f": f(ffn_w_gu), "wdn_f": f(ffn_w_down),
    }
    for si, S in enumerate((SP, SS)):
        ck, sk, cq, sq = _rope_tabs(S)
        shared.update({"ck%d" % si: ck, "sk%d" % si: sk, "cq%d" % si: cq, "sq%d" % si: sq})
    cp_, cs_ = f(c_prompt), f(c_sample)
    in_maps = []
    for c in range(NCORES):
        m = dict(shared)
        m["xp"] = x_prompt[c]
        m["xs"] = x_sample[c]
        cc = np.stack([cp_[c], cs_[c]], 0)
        m["cT"] = np.ascontiguousarray(cc.reshape(2, KD, 128).transpose(2, 1, 0))
        in_maps.append(m)
    res = run_bass_kernel_spmd(nc, in_maps, core_ids=list(range(NCORES)))
    yp = np.stack([np.asarray(res.results[c]["yp"], dtype=np.float32) for c in range(NCORES)], 0)
    ys = np.stack([np.asarray(res.results[c]["ys"], dtype=np.float32) for c in range(NCORES)], 0)
    return yp, ys
```
